# Optimizing a Trainium2 kernel written in Bass

```python
import jax, jax.numpy as jnp
from jax import lax
import numpy as np

D_MODEL = 2048
BATCH = 2
SEQ = 16384
DEPTH = 1

CHUNK = 64
Q_BLOCK = 128
N_MEM = 256
EPS = 1e-6

MLA_V = 128
MLA_NOPE = 128
MLA_ROPE = 64
MLA_HEADS = (D_MODEL // 2) // MLA_V
MLA_Q_RANK = 512
MLA_KV_RANK = 256
ROPE_THETA = 10000.0

HG_DK = 128
HG_DV = 128
HG_HEADS = (D_MODEL // 2) // HG_DV

MLA_WIDTH = MLA_HEADS * MLA_V
HG_WIDTH = HG_HEADS * HG_DV
MIX_WIDTH = MLA_WIDTH + HG_WIDTH
IN_SPLITS = (MLA_Q_RANK, MLA_KV_RANK, MLA_ROPE, HG_HEADS * HG_DK, HG_HEADS * HG_DK, HG_WIDTH, HG_WIDTH)
IN_WIDTH = sum(IN_SPLITS)

X_HEADS = 4
X_DIM = D_MODEL // X_HEADS

D_FF = 11 * D_MODEL // 4
CONV_WIDTH = 3

kernel_name = "hybrid_mla_hgrn2_streaming_layer"


def rms_norm(x, g):
    xf = x.astype(jnp.float32)
    y = xf * lax.rsqrt(jnp.mean(xf * xf, axis=-1, keepdims=True) + EPS)
    return (y * g.astype(jnp.float32)).astype(x.dtype)


def apply_rope(x, cos, sin):
    xf = x.astype(jnp.float32)
    x1, x2 = jnp.split(xf, 2, axis=-1)
    return jnp.concatenate([x1 * cos - x2 * sin, x2 * cos + x1 * sin], axis=-1).astype(x.dtype)


def chunk_causal_attention(q, k, v, scale):
    B, S, H, Dq = q.shape
    nb = S // Q_BLOCK
    qb = q.reshape(B, nb, Q_BLOCK, H, Dq).transpose(1, 0, 2, 3, 4)
    key_chunk = jnp.arange(S) // CHUNK
    neg = jnp.finfo(jnp.float32).min

    def one_block(args):
        qi, i = args
        q_chunk = (i * Q_BLOCK + jnp.arange(Q_BLOCK)) // CHUNK
        mask = key_chunk[None, :] <= q_chunk[:, None]
        s = jnp.einsum('bqhd,bkhd->bhqk', qi, k, preferred_element_type=jnp.float32) * scale
        s = jnp.where(mask[None, None], s, neg)
        p = jax.nn.softmax(s, axis=-1).astype(v.dtype)
        return jnp.einsum('bhqk,bkhd->bqhd', p, v)

    o = lax.map(one_block, (qb, jnp.arange(nb)))
    return o.transpose(1, 0, 2, 3, 4).reshape(B, S, H, v.shape[-1])


def mla_group(c_q, c_kv, k_r, q_norm, w_uq, kv_norm, w_ukv, cos, sin):
    B, S, _ = c_q.shape
    q = (rms_norm(c_q, q_norm) @ w_uq).reshape(B, S, MLA_HEADS, MLA_NOPE + MLA_ROPE)
    q_nope, q_rope = q[..., :MLA_NOPE], q[..., MLA_NOPE:]
    q_rope = apply_rope(q_rope, cos[None, :, None, :], sin[None, :, None, :])
    kv = (rms_norm(c_kv, kv_norm) @ w_ukv).reshape(B, S, MLA_HEADS, MLA_NOPE + MLA_V)
    k_nope, v = kv[..., :MLA_NOPE], kv[..., MLA_NOPE:]
    k_rope = apply_rope(k_r, cos[None], sin[None])
    k_rope = jnp.broadcast_to(k_rope[:, :, None, :], (B, S, MLA_HEADS, MLA_ROPE))
    qf = jnp.concatenate([q_nope, q_rope], axis=-1)
    kf = jnp.concatenate([k_nope, k_rope], axis=-1)
    scale = (MLA_NOPE + MLA_ROPE) ** -0.5
    o = chunk_causal_attention(qf, kf, v, scale)
    return o.reshape(B, S, MLA_WIDTH)


def hgrn2_group(hq, hf, hi, hg, lb, out_gain):
    B, S, _ = hq.shape
    f32 = jnp.float32
    q = jax.nn.silu(hq.astype(f32))
    f = lb + (1.0 - lb) * jax.nn.sigmoid(hf.astype(f32))
    k = 1.0 - f
    logf = jnp.log(f)
    v = hi.astype(f32)
    nc = S // CHUNK

    def to_chunks(t, d):
        return t.reshape(B, nc, CHUNK, HG_HEADS, d).transpose(1, 0, 3, 2, 4)

    qc, kc, gc = to_chunks(q, HG_DK), to_chunks(k, HG_DK), to_chunks(logf, HG_DK)
    vc = to_chunks(v, HG_DV)
    tri = jnp.tril(jnp.ones((CHUNK, CHUNK), dtype=bool))

    def body(state, inp):
        qi, ki, vi, gi = inp
        b = jnp.cumsum(gi, axis=2)
        diff = b[:, :, :, None, :] - b[:, :, None, :, :]
        decay = jnp.exp(jnp.where(tri[None, None, :, :, None], diff, -jnp.inf))
        a = jnp.einsum('bhtd,bhsd,bhtsd->bhts', qi, ki, decay)
        o = jnp.einsum('bhts,bhsv->bhtv', a, vi) + jnp.einsum('bhtd,bhdv->bhtv', qi * jnp.exp(b), state)
        b_last = b[:, :, -1:, :]
        state = jnp.exp(b_last[:, :, 0, :])[..., None] * state + jnp.einsum(
            'bhsd,bhsv->bhdv', ki * jnp.exp(b_last - b), vi)
        return state, o

    s0 = jnp.zeros((B, HG_HEADS, HG_DK, HG_DV), f32)
    _, o = lax.scan(body, s0, (qc, kc, vc, gc))
    o = o.transpose(1, 0, 3, 2, 4).reshape(B, S, HG_HEADS, HG_DV)
    o = rms_norm(o, out_gain) * jax.nn.silu(hg.astype(f32).reshape(B, S, HG_HEADS, HG_DV))
    return o.reshape(B, S, HG_WIDTH).astype(hq.dtype)


def causal_dwconv(u, w, b):
    C = u.shape[-1]
    y = lax.conv_general_dilated(u, w[:, None, :].astype(u.dtype), window_strides=(1,),
                                 padding=[(CONV_WIDTH - 1, 0)],
                                 dimension_numbers=('NWC', 'WIO', 'NWC'),
                                 feature_group_count=C)
    return y + b.astype(u.dtype)


def setup_inputs(seed: int = 0) -> dict:
    key = jax.random.key(seed)
    ks = jax.random.split(key, 32)
    f32 = jnp.float32

    def w(k, shape, fan_in):
        return jax.random.normal(k, shape, f32) * (fan_in ** -0.5)

    def gain(k, shape):
        return 1.0 + 0.02 * jax.random.normal(k, shape, f32)

    L = DEPTH
    return {
        "x": jax.random.normal(ks[0], (BATCH, SEQ, D_MODEL), f32),
        "mem": jax.random.normal(ks[1], (BATCH, N_MEM, D_MODEL), f32),
        "w_in": w(ks[2], (L, D_MODEL, IN_WIDTH), D_MODEL),
        "q_norm": gain(ks[3], (L, MLA_Q_RANK)),
        "w_uq": w(ks[4], (L, MLA_Q_RANK, MLA_HEADS * (MLA_NOPE + MLA_ROPE)), MLA_Q_RANK),
        "kv_norm": gain(ks[5], (L, MLA_KV_RANK)),
        "w_ukv": w(ks[6], (L, MLA_KV_RANK, MLA_HEADS * (MLA_NOPE + MLA_V)), MLA_KV_RANK),
        "mla_out_norm": gain(ks[7], (L, MLA_WIDTH)),
        "hgrn_lb": 0.5 * jax.random.normal(ks[8], (L + 1, HG_HEADS * HG_DK), f32),
        "hgrn_out_norm": gain(ks[9], (L, HG_DV)),
        "w_out": w(ks[10], (L, MIX_WIDTH, D_MODEL), MIX_WIDTH),
        "ln_mix_pre": gain(ks[11], (L, D_MODEL)),
        "ln_mix_post": gain(ks[12], (L, D_MODEL)),
        "ln_x_pre": gain(ks[13], (L, D_MODEL)),
        "ln_x_post": gain(ks[14], (L, D_MODEL)),
        "mem_norm": gain(ks[15], (L, D_MODEL)),
        "w_xq": w(ks[16], (L, D_MODEL, D_MODEL), D_MODEL),
        "w_xk": w(ks[17], (L, D_MODEL, D_MODEL), D_MODEL),
        "w_xv": w(ks[18], (L, D_MODEL, D_MODEL), D_MODEL),
        "w_xo": w(ks[19], (L, D_MODEL, D_MODEL), D_MODEL),
        "ln_ffn_pre": gain(ks[20], (L, D_MODEL)),
        "ln_ffn_post": gain(ks[21], (L, D_MODEL)),
        "w_up": w(ks[22], (L, D_MODEL, 2 * D_FF), D_MODEL),
        "conv_w": w(ks[23], (L, CONV_WIDTH, 2 * D_FF), CONV_WIDTH),
        "conv_b": 0.01 * jax.random.normal(ks[24], (L, 2 * D_FF), f32),
        "w_down": w(ks[25], (L, D_FF, D_MODEL), D_FF),
    }


def reference(x, mem, w_in, q_norm, w_uq, kv_norm, w_ukv, mla_out_norm, hgrn_lb, hgrn_out_norm,
              w_out, ln_mix_pre, ln_mix_post, ln_x_pre, ln_x_post, mem_norm, w_xq, w_xk, w_xv, w_xo,
              ln_ffn_pre, ln_ffn_post, w_up, conv_w, conv_b, w_down):
    B, S, D = x.shape
    M = mem.shape[1]
    f32 = jnp.float32
    split_idx = np.cumsum(np.array(IN_SPLITS))[:-1].tolist()

    pos = jnp.arange(S, dtype=f32)
    inv_freq = 1.0 / (ROPE_THETA ** (jnp.arange(0, MLA_ROPE, 2, dtype=f32) / MLA_ROPE))
    ang = pos[:, None] * inv_freq[None, :]
    cos, sin = jnp.cos(ang), jnp.sin(ang)

    lb_all = jnp.cumsum(jax.nn.softmax(hgrn_lb.astype(f32), axis=0), axis=0)

    h = x
    for l in range(DEPTH):
        xn = rms_norm(h, ln_mix_pre[l])
        z = xn @ w_in[l]
        c_q, c_kv, k_r, hq, hf, hi, hg = jnp.split(z, split_idx, axis=-1)
        a = mla_group(c_q, c_kv, k_r, q_norm[l], w_uq[l], kv_norm[l], w_ukv[l], cos, sin)
        a = rms_norm(a, mla_out_norm[l])
        r = hgrn2_group(hq, hf, hi, hg, lb_all[l], hgrn_out_norm[l])
        y = jnp.concatenate([a, r], axis=-1) @ w_out[l]
        h = h + rms_norm(y, ln_mix_post[l])

        xn = rms_norm(h, ln_x_pre[l])
        mn = rms_norm(mem, mem_norm[l])
        qx = (xn @ w_xq[l]).reshape(B, S, X_HEADS, X_DIM)
        kx = (mn @ w_xk[l]).reshape(B, M, X_HEADS, X_DIM)
        vx = (mn @ w_xv[l]).reshape(B, M, X_HEADS, X_DIM)
        s = jnp.einsum('bqhd,bkhd->bhqk', qx, kx, preferred_element_type=f32) * (X_DIM ** -0.5)
        p = jax.nn.softmax(s, axis=-1).astype(vx.dtype)
        ox = jnp.einsum('bhqk,bkhd->bqhd', p, vx).reshape(B, S, D) @ w_xo[l]
        h = h + rms_norm(ox, ln_x_post[l])

        xn = rms_norm(h, ln_ffn_pre[l])
        u = causal_dwconv(xn @ w_up[l], conv_w[l], conv_b[l])
        gate, val = jnp.split(u, 2, axis=-1)
        yf = (jax.nn.gelu(gate, approximate=True) * val) @ w_down[l]
        h = h + rms_norm(yf, ln_ffn_post[l])
    return h
```

```python
import numpy as np
import concourse.bass as bass
import concourse.mybir as mybir
from concourse.bass_utils import run_bass_kernel_spmd
from contextlib import ExitStack

F32 = mybir.dt.float32
BF16 = mybir.dt.bfloat16
AF = mybir.ActivationFunctionType
ALU = mybir.AluOpType

D = 2048
KC = 16
DFF = 5632
NFC = 44
EPS = 1e-6
HT = 130
C_Q, C_KV, C_KR, C_HQ, C_HF, C_HI, C_HG = 0, 512, 768, 832, 1856, 2880, 3904

G_MIXPRE, G_MIXPOST, G_XPRE, G_XPOST, G_MEM, G_FFNPRE, G_FFNPOST = 0, 16, 32, 48, 64, 80, 96
G_QN, G_KVN, G_MLAO, G_HGO, G_LB0, G_LB1 = 112, 116, 118, 126, 127, 135
G_CW, G_CB = 143, 143 + 264
NGN = 143 + 264 + 88
K_ID, K_CM, K_TRI, K_DM, K_KB, K_H0 = 0, 128, 640, 768, 898, 902
NCST = 903


class T:
    __slots__ = ("ap", "name", "lw", "rd", "dsem", "dcnt")

    def __init__(self, ap, name):
        self.ap = ap
        self.name = name
        self.lw = None
        self.rd = {}
        self.dsem = None
        self.dcnt = 0

    def __getitem__(self, idx):
        return self.ap[idx]


class MK:
    ROLL = 16000

    def __init__(self, nc):
        self.nc = nc
        self.eng = {"pe": nc.tensor, "act": nc.scalar, "dve": nc.vector, "pool": nc.gpsimd, "sp": nc.sync}
        self.sem = {k: nc.alloc_semaphore(f"s_{k}") for k in self.eng}
        self.allsem = {k: [self.sem[k]] for k in self.eng}
        self.cnt = {k: 0 for k in self.eng}
        self.seen = {k: {} for k in self.eng}
        self.ninst = {k: 0 for k in self.eng}
        self.nwait = {k: 0 for k in self.eng}
        self.dtiles = []
        self._uid = 0
        self.stack = None

    def sb(self, shape, dtype, name="t"):
        self._uid += 1
        name = f"{name}_{self._uid}"
        h = self.stack.enter_context(self.nc.sbuf_tensor(name, list(shape), dtype))
        return T(h.ap(), name)

    def ps(self, shape, dtype=F32, name="p"):
        self._uid += 1
        name = f"{name}_{self._uid}"
        h = self.stack.enter_context(self.nc.psum_tensor(name, list(shape), dtype))
        return T(h.ap(), name)

    def dram(self, name, shape, dtype, kind="Internal"):
        return T(self.nc.dram_tensor(name, list(shape), dtype, kind=kind).ap(), name)

    def _wait(self, e, sem, val):
        seen = self.seen[e]
        if seen.get(sem, 0) >= val:
            return
        self.eng[e].wait_ge(sem, val)
        self.nwait[e] += 1
        seen[sem] = val

    def op(self, e, fn, reads=(), writes=(), dma=False, nowaw=False):
        deps = {}

        def add(d):
            if d is None:
                return
            s, v = d
            if deps.get(s, 0) < v:
                deps[s] = v

        for t in reads:
            add(t.lw)
        for t in writes:
            if not nowaw:
                add(t.lw)
            for s, v in t.rd.items():
                add((s, v))
        for s, v in deps.items():
            if e == "pe" and any(s is o for o in self.allsem["pe"]):
                continue
            self._wait(e, s, v)
        ins = fn(self.eng[e])
        self.ninst[e] += 1
        if dma:
            t = writes[0]
            if t.dsem is None:
                t.dsem = self.nc.alloc_semaphore(f"d_{t.name}")
                self.dtiles.append(t)
            t.dcnt += 16
            ins.then_inc(t.dsem, 16)
            tok = (t.dsem, t.dcnt)
        else:
            if self.cnt[e] >= self.ROLL:
                self.sem[e] = self.nc.alloc_semaphore(f"s_{e}_{self.ninst[e]}")
                self.allsem[e].append(self.sem[e])
                self.cnt[e] = 0
            self.cnt[e] += 1
            ins.then_inc(self.sem[e], 1)
            tok = (self.sem[e], self.cnt[e])
        for t in writes:
            t.lw = tok
            t.rd = {}
        s, v = tok
        for t in reads:
            if t.rd.get(s, 0) < v:
                t.rd[s] = v
        return tok

    def barrier(self):
        toks = [(self.sem[k], self.cnt[k]) for k in self.eng if self.cnt[k] > 0]
        toks += [(t.dsem, t.dcnt) for t in self.dtiles]
        for e in self.eng:
            for s, v in toks:
                self._wait(e, s, v)

    def release_dsems(self, tiles):
        for t in tiles:
            if t.dsem is not None:
                self.dtiles.remove(t)
                t.dsem = None


class Rot:
    def __init__(self, items):
        self.items = items
        self.i = 0

    def next(self):
        x = self.items[self.i % len(self.items)]
        self.i += 1
        return x


def build(NG, dbg=False):
    S = NG * 512
    NO = NG * HT
    NT = NG * 4
    nc = bass.Bass("TRN2", target_bir_lowering=False)
    mk = MK(nc)
    I = "ExternalInput"
    xT = mk.dram("xT", [D, S], F32, I)
    memT = mk.dram("memT", [D, 256], F32, I)
    cs = mk.dram("cs", [2, 64, S], F32, I)
    cst = mk.dram("cst", [128, NCST], F32, I)
    gains = mk.dram("gains", [128, NGN], F32, I)
    w_in = mk.dram("w_in", [D, 4928], F32, I)
    w_uq = mk.dram("w_uq", [512, 1536], F32, I)
    w_ukv = mk.dram("w_ukv", [256, 2048], F32, I)
    w_out = mk.dram("w_out", [D, D], F32, I)
    w_xq = mk.dram("w_xq", [D, D], F32, I)
    w_xk = mk.dram("w_xk", [D, D], F32, I)
    w_xv = mk.dram("w_xv", [D, D], F32, I)
    w_xo = mk.dram("w_xo", [D, D], F32, I)
    w_up = mk.dram("w_up", [D, 2 * DFF], F32, I)
    w_down = mk.dram("w_down", [DFF, D], F32, I)
    outT = mk.dram("outT", [D, NG * 128], F32, "ExternalOutput")
    SK = "ExternalOutput" if dbg else "Internal"
    in_cols = ([C_KV, C_KV + 128] + [C_HF + 128 * i for i in range(8)] + [C_HI + 128 * i for i in range(8)]
               + [C_Q + 128 * i for i in range(4)] + [C_HQ + 128 * i for i in range(8)] + [C_HG + 128 * i for i in range(8)])
    NIC = len(in_cols)
    Win_b = mk.dram("Win_b", [NIC, 128, KC, 128], BF16)
    Wout_b = mk.dram("Wout_b", [16, 128, KC, 128], BF16)
    Wxq_b = mk.dram("Wxq_b", [16, 128, KC, 128], BF16)
    Wxk_b = mk.dram("Wxk_b", [16, 128, KC, 128], BF16)
    Wxv_b = mk.dram("Wxv_b", [4, 128, KC, 512], BF16)
    Wxo_b = mk.dram("Wxo_b", [16, 128, KC, 128], BF16)
    Wup_b = mk.dram("Wup_b", [88, 128, KC, 128], BF16)
    Wdn_b = mk.dram("Wdn_b", [16, 128, NFC, 128], BF16)
    KTs = mk.dram("KTs", [8, 128, S], BF16, SK)
    KRs = mk.dram("KRs", [64, S], BF16, SK)
    VTs = mk.dram("VTs", [8, 128, NT, 128], BF16, SK)
    QNs = mk.dram("QNs", [8, 128, NO], BF16, SK)
    QRs = mk.dram("QRs", [8, 64, NO], BF16, SK)
    Rs = mk.dram("Rs", [128, 8, NO], BF16, SK)
    As = mk.dram("As", [128, 8, NO], BF16, SK)

    def dma(q, out_ap, in_ap, reads, wt, nowaw=True):
        mk.op(q, lambda e: e.dma_start(out=out_ap, in_=in_ap), reads=reads, writes=[wt], dma=True, nowaw=nowaw)

    def conv_w(W, Wb, cols, kc):
        for oc, c0 in enumerate(cols):
            dma("pool", Wb[oc], W[:, c0:c0 + 128].rearrange("(kc p) n -> p kc n", p=128), [W], Wb)

    conv_w(w_in, Win_b, in_cols, KC)
    conv_jobs = []
    for W, Wb in ((w_out, Wout_b), (w_xq, Wxq_b), (w_xk, Wxk_b), (w_xo, Wxo_b)):
        for oc in range(16):
            conv_jobs.append((Wb, Wb[oc], W, W[:, oc * 128:(oc + 1) * 128].rearrange("(kc p) n -> p kc n", p=128)))
    for ob in range(4):
        for kh in range(2):
            conv_jobs.append((Wxv_b, Wxv_b[ob, :, 8 * kh:8 * kh + 8, :], w_xv,
                              w_xv[1024 * kh:1024 * (kh + 1), ob * 512:(ob + 1) * 512].rearrange("(kc p) n -> p kc n", p=128)))
    for oc in range(88):
        conv_jobs.append((Wup_b, Wup_b[oc], w_up, w_up[:, oc * 128:(oc + 1) * 128].rearrange("(kc p) n -> p kc n", p=128)))
    for oc in range(16):
        for ch in range(2):
            conv_jobs.append((Wdn_b, Wdn_b[oc, :, 22 * ch:22 * ch + 22, :], w_down,
                              w_down[2816 * ch:2816 * (ch + 1), oc * 128:(oc + 1) * 128].rearrange("(kc p) n -> p kc n", p=128)))
    per_g = -(-len(conv_jobs) // NG)

    def emit_conv(n):
        for _ in range(n):
            if conv_jobs:
                Wb, o_ap, W, i_ap = conv_jobs.pop(0)
                dma("pool", o_ap, i_ap, [W], Wb)

    with ExitStack() as es_all:
        mk.stack = es_all
        cst_f = mk.sb([128, NCST], F32, "cst_f")
        gn = mk.sb([128, NGN], F32, "gn")
        dma("sp", cst_f[:], cst[:], [cst], cst_f)
        dma("sp", gn[:], gains[:], [gains], gn)
        ident_b = mk.sb([128, 128], BF16, "ident_b")
        ones_b = mk.sb([128, 128], BF16, "ones_b")
        tri_b = mk.sb([128, 128], BF16, "tri_b")
        dmask_b = mk.sb([128, HT], BF16, "dmask_b")
        mk.op("dve", lambda e: e.tensor_copy(ident_b[:], cst_f[:, K_ID:K_ID + 128]), [cst_f], [ident_b])
        mk.op("dve", lambda e: e.memset(ones_b[:], 1.0), [], [ones_b])
        mk.op("dve", lambda e: e.tensor_copy(tri_b[:], cst_f[:, K_TRI:K_TRI + 128]), [cst_f], [tri_b])
        mk.op("dve", lambda e: e.tensor_copy(dmask_b[:], cst_f[:, K_DM:K_DM + HT]), [cst_f], [dmask_b])
        lb = mk.sb([128, 8], F32, "lb")
        oml = mk.sb([128, 8], F32, "oml")
        mk.op("dve", lambda e: e.tensor_tensor(lb[:], gn[:, G_LB0:G_LB0 + 8], gn[:, G_LB1:G_LB1 + 8], ALU.subtract), [gn], [lb])
        mk.op("act", lambda e: e.activation(lb[:], lb[:], AF.Sigmoid), [lb], [lb])
        mk.op("dve", lambda e: e.tensor_scalar(oml[:], lb[:], -1.0, 1.0, ALU.mult, ALU.add), [lb], [oml])

        def rstd_from_ps(ps_t, ps_ap, out_t, out_ap, tmp_t, tmp_ap, inv_n):
            mk.op("dve", lambda e: e.tensor_scalar(tmp_ap, ps_ap, inv_n, EPS, ALU.mult, ALU.add), [ps_t], [tmp_t])
            mk.op("act", lambda e: e.activation(tmp_ap, tmp_ap, AF.Sqrt), [tmp_t], [tmp_t])
            mk.op("dve", lambda e: e.reciprocal(out_ap, tmp_ap), [tmp_t], [out_t])

        with ExitStack() as es:
            mk.stack = es
            A_tiles = []
            banks = [mk.ps([128, 512], F32, f"bk{i}") for i in range(7)]
            ps_tr = mk.ps([128, 1024], BF16, "ps_tr")
            pproj = Rot(banks[0:2])
            ps_n = banks[2]
            pst = Rot(banks[3:5])
            pbo2 = Rot(banks[5:7])
            wkr = mk.sb([128, KC, 64], BF16, "wkr")
            wkrr = mk.sb([128, KC, 64], BF16, "wkrr")
            dma("pool", wkr[:], w_in[:, C_KR:C_KR + 64].rearrange("(kc p) n -> p kc n", p=128), [w_in], wkr)
            mk.op("dve", lambda e: e.tensor_scalar(wkrr[:, :, 0:32], wkr[:, :, 32:64], -1.0, None, ALU.mult), [wkr], [wkrr])
            mk.op("dve", lambda e: e.tensor_copy(wkrr[:, :, 32:64], wkr[:, :, 0:32]), [wkr], [wkrr])
            wuq = mk.sb([128, 4, 1536], BF16, "wuq")
            dma("pool", wuq[:], w_uq[:].rearrange("(kc p) n -> p kc n", p=128), [w_uq], wuq)
            wuqr = mk.sb([128, 4, 8, 64], BF16, "wuqr")
            wuq_v = wuq[:].rearrange("p k (h d) -> p k h d", d=192)
            mk.op("dve", lambda e: e.tensor_scalar(wuqr[:, :, :, 0:32], wuq_v[:, :, :, 160:192], -1.0, None, ALU.mult), [wuq], [wuqr])
            mk.op("dve", lambda e: e.tensor_copy(wuqr[:, :, :, 32:64], wuq_v[:, :, :, 128:160]), [wuq], [wuqr])
            wukv = mk.sb([128, 2, 2048], BF16, "wukv")
            dma("pool", wukv[:], w_ukv[:].rearrange("(kc p) n -> p kc n", p=128), [w_ukv], wukv)
            wukv_v = wukv[:].rearrange("p k (h t d) -> p k h t d", t=2, d=128)
            xq = Rot([mk.sb([128, 2, 512], F32, f"xq{i}") for i in range(2)])
            sqq = Rot([mk.sb([128, 2, 512], BF16, f"sq{i}") for i in range(2)])
            xgs = Rot([mk.sb([128, KC, 512], BF16, f"xg{i}") for i in range(2)])
            wst = Rot([mk.sb([128, KC, 128], BF16, f"wst{i}") for i in range(3)])
            rstd = mk.sb([128, 512], F32, "rstd")
            ckv = mk.sb([128, 2, 512], F32, "ckv")
            sqtmp = mk.sb([128, 8 * HT], BF16, "sqtmp")
            ckvsq_v = sqtmp[:, 0:1024].rearrange("p (c s) -> p c s", c=2)
            o_sq_v = sqtmp[:, :].rearrange("p (h t) -> p h t", t=HT)
            cq_sq_v = sqtmp[:, 0:4 * HT].rearrange("p (h t) -> p h t", t=HT)
            rkv = mk.sb([128, 512], F32, "rkv")
            tmpn = rkv
            ckvn = mk.sb([128, 2, 512], BF16, "ckvn")
            kt_sb = mk.sb([128, 4, 512], BF16, "kt_sb")
            vm_sb = mk.sb([128, 4, 4, 128], BF16, "vm_sb")
            kr_f = mk.sb([64, 512], F32, "kr_f")
            krr_f = mk.sb([64, 512], F32, "krr_f")
            cos_t = mk.sb([64, 512], F32, "cos_t")
            sin_t = mk.sb([64, 512], F32, "sin_t")
            kro = mk.sb([64, 512], BF16, "kro")
            Fh = mk.sb([128, 4, 512], F32, "Fh")
            Lh = mk.sb([128, 4, 512], F32, "Lh")
            Bh = mk.sb([128, 4, 512], F32, "Bh")
            ebl = mk.sb([128, 4, 8], F32, "ebl")
            kdl = mk.sb([128, 4, 512], BF16, "kdl")
            vT = mk.sb([128, 4, 512], BF16, "vT")
            kdl_tok = mk.sb([128, 4, 4, 128], BF16, "kdl_tok")
            v_tok = mk.sb([128, 4, 4, 128], BF16, "v_tok")
            hq_f = mk.sb([128, 4, 256], F32, "hq_f")
            qd = mk.sb([128, 4, 256], BF16, "qd")
            kdo = mk.sb([128, 4, 256], BF16, "kdo")
            at_sb = Rot([mk.sb([128, 128], BF16, f"at{i}") for i in range(2)])
            st_f = [mk.sb([128, 128], F32, f"stf{h}") for h in range(8)]
            st_snap = [mk.sb([128, 4, 128], BF16, f"stb{h}") for h in range(8)]
            o_f = mk.sb([128, 8, HT], F32, "o_f")
            ro = mk.sb([128, 8, HT], F32, "ro")
            hg_f = mk.sb([128, 8, HT], F32, "hg_f")
            r_b = mk.sb([128, 8, HT], BF16, "r_b")
            cq_f = mk.sb([128, 4, HT], F32, "cq_f")
            rq = mk.sb([128, HT], F32, "rq")
            cqn = mk.sb([128, 4, HT], BF16, "cqn")
            qn_sb = mk.sb([128, 8, HT], BF16, "qn_sb")
            qr_f = mk.sb([64, 3, HT], F32, "qr_f")
            qrr_f = mk.sb([64, 3, HT], F32, "qrr_f")
            qr_sb = mk.sb([64, 8, HT], BF16, "qr_sb")
            for h in range(8):
                mk.op("dve", lambda e, h=h: e.memset(st_f[h][:], 0.0), [], [st_f[h]])
            cmask = cst_f[:, K_CM:K_CM + 512]
            wi = [0]

            def load_w(ci):
                t = wst.next()
                dma("sp", t[:], Win_b[ci], [Win_b], t, nowaw=False)
                return t

            def proj(ci, xg, c0, c1, M=128, wt=None, wap=None):
                if wt is None:
                    wt = load_w(ci)
                    wap = lambda kc: wt[:, kc, :]
                bk = pproj.next()
                n = c1 - c0
                for kc in range(KC):
                    mk.op("pe", lambda e, kc=kc: e.matmul(bk[0:M, 0:n], wap(kc), xg[:, kc, c0:c1], start=(kc == 0), stop=(kc == KC - 1)),
                          [wt, xg], [bk])
                return bk, bk[0:M, 0:n]

            accx = mk.sb([128, 512], F32, "accx")
            accxb = sqtmp[:, 0:512]
            xg_of = {}

            xTv_a = xT[:].rearrange("(kc p) s -> p kc s", p=128)
            xbuf = xq.items
            xstate = {"n": 0}
            NCH = NG * 8

            def x_issue(n):
                if n >= NCH:
                    return
                g, q = divmod(n, 8)
                xt = xbuf[n % 2]
                dma("pool", xt[:], xTv_a[:, 2 * q:2 * q + 2, g * 512:g * 512 + 512], [xT], xt, nowaw=False)

            def x_consume(n):
                g, q = divmod(n, 8)
                if q == 0:
                    xg_of[g] = xgs.next()
                xg = xg_of[g]
                xt = xbuf[n % 2]
                sq = sqq.next()
                mk.op("act", lambda e, xt=xt, sq=sq: e.activation(sq[:], xt[:], AF.Square), [xt], [sq])
                for k in range(2):
                    kc = 2 * q + k
                    if kc == 0:
                        mk.op("pool", lambda e, sq=sq, k=k: e.tensor_copy(accx[:], sq[:, k, :]), [sq], [accx])
                    else:
                        mk.op("pool", lambda e, sq=sq, k=k: e.tensor_tensor(accx[:], accx[:], sq[:, k, :], ALU.add), [sq, accx], [accx])
                    mk.op("dve", lambda e, xt=xt, k=k, kc=kc, xg=xg: e.tensor_scalar(xg[:, kc, :], xt[:, k, :], gn[:, G_MIXPRE + kc:G_MIXPRE + kc + 1], None, ALU.mult),
                          [xt, gn], [xg])

            def xstep():
                n = xstate["n"]
                if n >= NCH:
                    return
                x_consume(n)
                x_issue(n + 2)
                xstate["n"] = n + 1

            class Filler:
                def __init__(self, items, k=1):
                    self.items = list(items)
                    self.pending = []
                    self.k = k

                def step(self):
                    for ev in self.pending:
                        ev()
                    self.pending = []
                    for _ in range(self.k):
                        if self.items:
                            pe_fn, ev_fn = self.items.pop(0)
                            ctx = pe_fn()
                            self.pending.append(lambda ctx=ctx, ev_fn=ev_fn: ev_fn(ctx))

                def flush(self):
                    while self.items or self.pending:
                        self.step()

            x_issue(0)
            x_issue(1)
            for _ in range(8):
                xstep()
            for g in range(NG):
                s0 = g * 512
                xg = xg_of[g]
                mk.op("act", lambda e: e.activation(accxb, accx[:], AF.Copy), [accx], [sqtmp])
                mk.op("pe", lambda e: e.matmul(ps_n[:, :], ones_b[:], accxb, start=True, stop=True), [sqtmp, ones_b], [ps_n])
                rstd_from_ps(ps_n, ps_n[:, :], rstd, rstd[:], rstd, rstd[:], 1.0 / D)
                dma("sp", cos_t[:], cs[0, :, s0:s0 + 512], [cs], cos_t, nowaw=False)
                dma("sp", sin_t[:], cs[1, :, s0:s0 + 512], [cs], sin_t, nowaw=False)
                for c in range(2):
                    bk, ap = proj(c, xg, 0, 512)
                    mk.op("dve", lambda e, ap=ap, c=c: e.tensor_tensor(ckv[:, c, :], ap, rstd[:], ALU.mult), [bk, rstd], [ckv])
                bk, ap = proj(None, xg, 0, 512, M=64, wt=wkr, wap=lambda kc: wkr[:, kc, :])
                mk.op("dve", lambda e, ap=ap: e.tensor_tensor(kr_f[:], ap, rstd[0:64, :], ALU.mult), [bk, rstd], [kr_f])
                bk, ap = proj(None, xg, 0, 512, M=64, wt=wkrr, wap=lambda kc: wkrr[:, kc, :])
                mk.op("dve", lambda e, ap=ap: e.tensor_tensor(krr_f[:], ap, rstd[0:64, :], ALU.mult), [bk, rstd], [krr_f])
                mk.op("dve", lambda e: e.tensor_tensor(kr_f[:], kr_f[:], cos_t[:], ALU.mult), [kr_f, cos_t], [kr_f])
                mk.op("dve", lambda e: e.tensor_tensor(krr_f[:], krr_f[:], sin_t[:], ALU.mult), [krr_f, sin_t], [krr_f])
                mk.op("dve", lambda e: e.tensor_tensor(kro[:], kr_f[:], krr_f[:], ALU.add), [kr_f, krr_f], [kro])
                dma("pool", KRs[:, s0:s0 + 512], kro[:], [kro], KRs)
                mk.op("act", lambda e: e.activation(ckvsq_v, ckv[:], AF.Square), [ckv], [sqtmp])
                for c in range(2):
                    mk.op("pe", lambda e, c=c: e.matmul(ps_n[:, :], ones_b[:], ckvsq_v[:, c, :], start=(c == 0), stop=(c == 1)), [sqtmp, ones_b], [ps_n])
                rstd_from_ps(ps_n, ps_n[:, :], rkv, rkv[:], rkv, rkv[:], 1.0 / 256)
                for c in range(2):
                    mk.op("dve", lambda e, c=c: e.scalar_tensor_tensor(ckvn[:, c, :], ckv[:, c, :], gn[:, G_KVN + c:G_KVN + c + 1], rkv[:], ALU.mult, ALU.mult),
                          [ckv, gn, rkv], [ckvn])
                def mk_k(hh2, i):
                    h = 4 * hh2 + i

                    def pe_fn():
                        bk = pproj.next()
                        for c in range(2):
                            mk.op("pe", lambda e, c=c, bk=bk: e.matmul(bk[:, :], wukv_v[:, c, h, 0, :], ckvn[:, c, :], start=(c == 0), stop=(c == 1)), [wukv, ckvn], [bk])
                        return bk

                    def ev_fn(bk):
                        mk.op("act", lambda e: e.activation(kt_sb[:, i, :], bk[:, :], AF.Copy), [bk], [kt_sb])
                        if i == 3:
                            dma("pool", KTs[4 * hh2:4 * hh2 + 4, :, s0:s0 + 512].rearrange("h p s -> p h s"), kt_sb[:], [kt_sb], KTs)
                    return pe_fn, ev_fn

                def mk_v(hh2, tt):
                    def pe_fn():
                        bk = pproj.next()
                        for c in range(2):
                            mk.op("pe", lambda e, c=c, bk=bk: e.matmul(bk[:, :].rearrange("p (h d) -> p h d", d=128), ckvn[:, c, tt * 128:(tt + 1) * 128],
                                                                     wukv_v[:, c, 4 * hh2:4 * hh2 + 4, 1, :], start=(c == 0), stop=(c == 1)), [wukv, ckvn], [bk])
                        return bk

                    def ev_fn(bk):
                        mk.op("act", lambda e: e.activation(vm_sb[:, :, tt, :], bk[:, :].rearrange("p (h d) -> p h d", d=128), AF.Copy), [bk], [vm_sb])
                        if tt == 3:
                            dma("pool", VTs[4 * hh2:4 * hh2 + 4, :, 4 * g:4 * g + 4, :].rearrange("h p t d -> p h t d"), vm_sb[:], [vm_sb], VTs)
                    return pe_fn, ev_fn

                fillkv = []
                for hh2 in range(2):
                    fillkv += [mk_k(hh2, i) for i in range(4)] + [mk_v(hh2, tt) for tt in range(4)]

                def mk_own_proj(ci, dst, idx):
                    def pe_fn():
                        return proj(ci, xg, 382, 512)

                    def ev_fn(ctx):
                        bk, ap = ctx
                        mk.op("dve", lambda e: e.tensor_tensor(dst[:, idx, :], ap, rstd[:, 382:512], ALU.mult), [bk, rstd], [dst])
                    return pe_fn, ev_fn

                fill0 = [mk_own_proj(30 + h, hg_f, h) for h in range(8)] + [mk_own_proj(18 + c, cq_f, c) for c in range(4)]

                def mk_qn(hs):
                    n = len(hs)

                    def pe_fn():
                        bk = pproj.next()
                        for k, h in enumerate(hs):
                            for c in range(4):
                                mk.op("pe", lambda e, k=k, h=h, c=c, bk=bk: e.matmul(bk[:, k * HT:(k + 1) * HT], wuq[:, c, h * 192:h * 192 + 128], cqn[:, c, :], start=(c == 0), stop=(c == 3)), [wuq, cqn], [bk])
                        return bk

                    def ev_fn(bk):
                        mk.op("act", lambda e: e.activation(qn_sb[:, hs[0]:hs[0] + n, :], bk[:, 0:n * HT].rearrange("p (h t) -> p h t", t=HT), AF.Copy), [bk], [qn_sb])
                    return pe_fn, ev_fn

                def mk_qr(hs, wsel):
                    n = len(hs)
                    dst = qr_f if wsel == 0 else qrr_f
                    cs_t = cos_t if wsel == 0 else sin_t

                    def pe_fn():
                        bk = pproj.next()
                        for k, h in enumerate(hs):
                            for c in range(4):
                                lw = (wuq[:, c, h * 192 + 128:h * 192 + 192] if wsel == 0 else wuqr[:, c, h, :])
                                mk.op("pe", lambda e, k=k, c=c, bk=bk, lw=lw: e.matmul(bk[0:64, k * HT:(k + 1) * HT], lw, cqn[:, c, :], start=(c == 0), stop=(c == 3)), [wuq, wuqr, cqn], [bk])
                        return bk

                    def ev_fn(bk):
                        mk.op("dve", lambda e: e.tensor_tensor(dst[:, 0:n, :], bk[0:64, 0:n * HT].rearrange("p (h t) -> p h t", t=HT),
                                                               cs_t[:, 382:512].unsqueeze(1).to_broadcast([64, n, HT]), ALU.mult), [bk, cs_t], [dst])
                        if wsel == 1:
                            mk.op("dve", lambda e: e.tensor_tensor(qr_sb[:, hs[0]:hs[0] + n, :], qr_f[:, 0:n, :], qrr_f[:, 0:n, :], ALU.add), [qr_f, qrr_f], [qr_sb])
                    return pe_fn, ev_fn

                fill1 = []
                for hb in range(3):
                    hs = list(range(3 * hb, min(3 * hb + 3, 8)))
                    fill1 += [mk_qn(hs), mk_qr(hs, 0), mk_qr(hs, 1)]

                for hh in range(2):
                    fl = Filler(fillkv + fill0, 2) if hh == 0 else Filler(fill1, 1)
                    for i in range(4):
                        bk, ap = proj(2 + 4 * hh + i, xg, 0, 512)
                        mk.op("dve", lambda e, ap=ap, i=i: e.tensor_tensor(Fh[:, i, :], ap, rstd[:], ALU.mult), [bk, rstd], [Fh])
                        if i % 2 == 0:
                            xstep()
                    for i in range(4):
                        bk, ap = proj(10 + 4 * hh + i, xg, 0, 512)
                        mk.op("dve", lambda e, ap=ap, i=i: e.tensor_tensor(vT[:, i, :], ap, rstd[:], ALU.mult), [bk, rstd], [vT])
                        if i % 2 == 0:
                            xstep()
                    for i in range(4):
                        bk, ap = proj(22 + 4 * hh + i, xg, 256, 512)
                        mk.op("dve", lambda e, ap=ap, i=i: e.tensor_tensor(hq_f[:, i, :], ap, rstd[:, 256:512], ALU.mult), [bk, rstd], [hq_f])
                    mk.op("act", lambda e: e.activation(Fh[:], Fh[:], AF.Sigmoid), [Fh], [Fh])
                    fl.step()
                    for i in range(4):
                        h = 4 * hh + i
                        mk.op("dve", lambda e, i=i, h=h: e.tensor_scalar(Fh[:, i, :], Fh[:, i, :], oml[:, h:h + 1], lb[:, h:h + 1], ALU.mult, ALU.add), [Fh, oml, lb], [Fh])
                        fl.step()
                    mk.op("act", lambda e: e.activation(Lh[:], Fh[:], AF.Ln), [Fh], [Lh])
                    for i in range(4):
                        mk.op("dve", lambda e, i=i: e.tensor_tensor_scan(Bh[:, i, :], cmask, Lh[:, i, :], 0.0, ALU.mult, ALU.add), [Lh, cst_f], [Bh])
                        fl.step()
                    mk.op("dve", lambda e: e.tensor_scalar(Fh[:], Fh[:], -1.0, 1.0, ALU.mult, ALU.add), [Fh], [Fh])
                    fl.step()
                    Bv = Bh[:].rearrange("p h (c t) -> p h c t", t=64)
                    mk.op("act", lambda e: e.activation(ebl[:], Bv[:, :, :, 63], AF.Exp), [Bh], [ebl])
                    mk.op("act", lambda e: e.activation(Lh[:], Bh[:], AF.Exp, scale=-1.0), [Bh], [Lh])
                    mk.op("dve", lambda e: e.tensor_tensor(Fh[:], Fh[:], Lh[:], ALU.mult), [Fh, Lh], [Fh])
                    fl.step()
                    mk.op("dve", lambda e: e.tensor_tensor(kdl[:].rearrange("p h (c t) -> p h c t", t=64), Fh[:].rearrange("p h (c t) -> p h c t", t=64),
                                                           ebl[:].unsqueeze(3).to_broadcast([128, 4, 8, 64]), ALU.mult), [Fh, ebl], [kdl])
                    fl.step()
                    mk.op("dve", lambda e: e.tensor_copy(kdo[:], Fh[:, :, 256:512]), [Fh], [kdo])
                    mk.op("act", lambda e: e.activation(Lh[:, :, 0:256], Bh[:, :, 256:512], AF.Exp), [Bh, kdl], [Lh])
                    mk.op("act", lambda e: e.activation(hq_f[:], hq_f[:], AF.Silu), [hq_f], [hq_f])
                    fl.step()
                    mk.op("dve", lambda e: e.tensor_tensor(qd[:], hq_f[:], Lh[:, :, 0:256], ALU.mult), [hq_f, Lh], [qd])
                    fl.flush()
                    if hh == 0:
                        mk.op("act", lambda e: e.activation(hg_f[:], hg_f[:], AF.Silu), [hg_f], [hg_f])
                        mk.op("act", lambda e: e.activation(cq_sq_v, cq_f[:], AF.Square), [cq_f], [sqtmp])
                        for c in range(4):
                            mk.op("pe", lambda e, c=c: e.matmul(ps_n[:, 0:HT], ones_b[:], cq_sq_v[:, c, :], start=(c == 0), stop=(c == 3)), [ones_b, sqtmp], [ps_n])
                        rstd_from_ps(ps_n, ps_n[:, 0:HT], rq, rq[:], rq, rq[:], 1.0 / 512)
                        for c in range(4):
                            mk.op("dve", lambda e, c=c: e.scalar_tensor_tensor(cqn[:, c, :], cq_f[:, c, :], gn[:, G_QN + c:G_QN + c + 1], rq[:], ALU.mult, ALU.mult), [cq_f, gn, rq], [cqn])
                    for src, dst in ((kdl, kdl_tok), (vT, v_tok)):
                        for tp in range(2):
                            for t2 in range(2):
                                tt = 2 * tp + t2
                                for i in range(4):
                                    mk.op("pe", lambda e, src=src, tt=tt, i=i, t2=t2: e.transpose(ps_tr[:, (t2 * 4 + i) * 128:(t2 * 4 + i + 1) * 128], src[:, i, tt * 128:(tt + 1) * 128], ident_b[:]),
                                          [src, ident_b], [ps_tr])
                            mk.op("act", lambda e, dst=dst, tp=tp: e.activation(dst[:, 2 * tp:2 * tp + 2, :, :], ps_tr[:, :].rearrange("p (t h d) -> p t h d", t=2, h=4), AF.Copy), [ps_tr], [dst])
                    for tt in range(4):
                        for c2 in range(2):
                            c = 2 * tt + c2
                            pr = slice(c2 * 64, c2 * 64 + 64)
                            bks = pst.next()
                            for i in range(4):
                                mk.op("pe", lambda e, i=i, tt=tt, pr=pr, bks=bks: e.matmul(bks[:, i * 128:(i + 1) * 128], kdl_tok[pr, tt, i, :], v_tok[pr, tt, i, :], start=True, stop=True), [kdl_tok, v_tok], [bks])
                            for i in range(4):
                                h = 4 * hh + i
                                if c >= 4:
                                    mk.op("act", lambda e, h=h, c=c: e.activation(st_snap[h][:, c - 4, :], st_f[h][:], AF.Copy), [st_f[h]], [st_snap[h]])
                                mk.op("dve", lambda e, h=h, i=i, c=c, bks=bks: e.scalar_tensor_tensor(st_f[h][:], st_f[h][:], ebl[:, i, c:c + 1], bks[:, i * 128:(i + 1) * 128], ALU.mult, ALU.add),
                                      [st_f[h], ebl, bks], [st_f[h]])
                    for i in range(4):
                        h = 4 * hh + i
                        for tt in (2, 3):
                            oc0 = (tt - 2) * 128
                            bka = pst.next()
                            mk.op("pe", lambda e, i=i, oc0=oc0, bka=bka: e.matmul(bka[:, 0:128], kdo[:, i, oc0:oc0 + 128], qd[:, i, oc0:oc0 + 128], start=True, stop=True), [kdo, qd], [bka])
                            at = at_sb.next()
                            mk.op("dve", lambda e, at=at, bka=bka: e.tensor_tensor(at[:], bka[:, 0:128], tri_b[:], ALU.mult), [bka, tri_b], [at])
                            bko = pbo2.next()
                            mk.op("pe", lambda e, i=i, tt=tt, at=at, bko=bko: e.matmul(bko[:, 0:128], v_tok[:, tt, i, :], at[:], start=True, stop=False), [v_tok, at], [bko])
                            for c2 in range(2):
                                mk.op("pe", lambda e, i=i, h=h, oc0=oc0, c2=c2, tt=tt, bko=bko: e.matmul(bko[:, c2 * 64:c2 * 64 + 64], st_snap[h][:, 2 * tt + c2 - 4, :], qd[:, i, oc0 + c2 * 64:oc0 + c2 * 64 + 64],
                                                                                                       start=False, stop=(c2 == 1)), [st_snap[h], qd], [bko])
                            if tt == 2:
                                mk.op("act", lambda e, h=h, bko=bko: e.activation(o_f[:, h, 0:2], bko[:, 126:128], AF.Copy), [bko], [o_f])
                            else:
                                mk.op("act", lambda e, h=h, bko=bko: e.activation(o_f[:, h, 2:HT], bko[:, 0:128], AF.Copy), [bko], [o_f])
                mk.op("act", lambda e: e.activation(o_sq_v, o_f[:], AF.Square), [o_f], [sqtmp])
                for hb in range(3):
                    hs = list(range(3 * hb, min(3 * hb + 3, 8)))
                    bk = pproj.next()
                    for k, h in enumerate(hs):
                        mk.op("pe", lambda e, k=k, h=h, bk=bk: e.matmul(bk[:, k * HT:(k + 1) * HT], ones_b[:], o_sq_v[:, h, :], start=True, stop=True), [ones_b, sqtmp], [bk])
                    n = len(hs)
                    rstd_from_ps(bk, bk[:, 0:n * HT].rearrange("p (h t) -> p h t", t=HT), ro, ro[:, hs[0]:hs[0] + n, :], ro, ro[:, hs[0]:hs[0] + n, :], 1.0 / 128)
                mk.op("dve", lambda e: e.scalar_tensor_tensor(o_f[:], o_f[:], gn[:, G_HGO:G_HGO + 1], ro[:], ALU.mult, ALU.mult), [o_f, gn, ro], [o_f])
                mk.op("dve", lambda e: e.tensor_tensor(r_b[:], o_f[:], hg_f[:], ALU.mult), [o_f, hg_f], [r_b])
                dma("pool", Rs[:, :, g * HT:(g + 1) * HT], r_b[:], [r_b], Rs)
                dma("pool", QNs[:, :, g * HT:(g + 1) * HT].rearrange("h p t -> p h t"), qn_sb[:], [qn_sb], QNs)
                dma("pool", QRs[:, :, g * HT:(g + 1) * HT].rearrange("h p t -> p h t"), qr_sb[:], [qr_sb], QRs)
                emit_conv(per_g)
            emit_conv(len(conv_jobs))
            mk.barrier()

        SC = 192.0 ** -0.5
        with ExitStack() as es:
            mk.stack = es
            pbs = Rot([mk.ps([128, 1024], F32, f"bs{i}") for i in range(2)])
            pbo2 = Rot([mk.ps([128, 512], F32, f"bo{i}") for i in range(2)])
            pbd2 = Rot([mk.ps([128, 512], F32, f"bd{i}") for i in range(2)])
            kt = mk.sb([128, S], BF16, "kt")
            vt = mk.sb([128, NT, 128], BF16, "vt")
            krs = mk.sb([128, S], BF16, "krs")
            qn = mk.sb([128, NO], BF16, "qn")
            qr = mk.sb([128, NO], BF16, "qr")
            a_h = mk.sb([128, NO], BF16, "a_h")
            W2 = 2 * HT
            pts = Rot([mk.sb([128, 2, W2], BF16, f"pt{i}") for i in range(4)])
            accs = Rot([mk.sb([128, 2, W2], F32, f"acc{i}") for i in range(2)])
            accps = Rot([mk.sb([128, 2, W2], F32, f"accp{i}") for i in range(2)])
            acc1s = Rot([mk.sb([128, 3, HT], F32, f"acd{i}") for i in range(2)])
            accb = Rot([mk.sb([128, 2, W2], BF16, f"accb{i}") for i in range(2)])
            acc1b = Rot([mk.sb([128, 3, HT], BF16, f"acdb{i}") for i in range(2)])
            rec = mk.sb([128, W2], F32, "rec")
            nseg = max(1, S // 4096)
            sw = S // nseg
            mk.op("dve", lambda e: e.memset(krs[64:128, :], 0.0), [], [krs])
            mk.op("dve", lambda e: e.memset(qr[64:128, :], 0.0), [], [qr])
            for sg in range(nseg):
                dma("sp", krs[0:64, sg * sw:(sg + 1) * sw], KRs[:, sg * sw:(sg + 1) * sw], [KRs], krs)
            for h in range(8):
                for sg in range(nseg):
                    dma("sp", kt[:, sg * sw:(sg + 1) * sw], KTs[h, :, sg * sw:(sg + 1) * sw], [KTs], kt, nowaw=(sg > 0))
                tw = NT // nseg
                for sg in range(nseg):
                    dma("sp", vt[:, sg * tw:(sg + 1) * tw, :], VTs[h, :, sg * tw:(sg + 1) * tw, :], [VTs], vt, nowaw=(sg > 0))
                dma("sp", qn[:], QNs[h], [QNs], qn, nowaw=False)
                dma("sp", qr[0:64, :], QRs[h], [QRs], qr, nowaw=False)
                items = []
                for u in range(NG // 2):
                    ncom = 8 * u + 4
                    for i in range(0, ncom, 2):
                        items.append((u, 0, [i, i + 1], i == 0, False))
                    items.append((u, 1, [ncom, ncom + 1, ncom + 2], False, False))
                    items.append((u, 1, [ncom + 3], False, True))

                def emit_qk(it):
                    u, kind, tiles, first, last = it
                    bs = pbs.next()
                    if kind == 0:
                        qs = slice(u * W2, (u + 1) * W2)
                        for i, ki in enumerate(tiles):
                            ks = slice(ki * 128, (ki + 1) * 128)
                            mk.op("pe", lambda e, bs=bs, i=i, ks=ks, qs=qs: e.matmul(bs[:, i * 512:i * 512 + W2], kt[:, ks], qn[:, qs], start=True, stop=False), [kt, qn], [bs])
                            mk.op("pe", lambda e, bs=bs, i=i, ks=ks, qs=qs: e.matmul(bs[:, i * 512:i * 512 + W2], krs[:, ks], qr[:, qs], start=False, stop=True), [krs, qr], [bs])
                    else:
                        qs = slice(u * W2 + HT, (u + 1) * W2)
                        for i, ki in enumerate(tiles):
                            ks = slice(ki * 128, (ki + 1) * 128)
                            mk.op("pe", lambda e, bs=bs, i=i, ks=ks, qs=qs: e.matmul(bs[:, i * HT:(i + 1) * HT], kt[:, ks], qn[:, qs], start=True, stop=False), [kt, qn], [bs])
                            mk.op("pe", lambda e, bs=bs, i=i, ks=ks, qs=qs: e.matmul(bs[:, i * HT:(i + 1) * HT], krs[:, ks], qr[:, qs], start=False, stop=True), [krs, qr], [bs])
                    return bs

                stE = {}
                stP = {}
                tailq = []

                def emit_exp(it, bs):
                    u, kind, tiles, first, last = it
                    ncom = 8 * u + 4
                    if first:
                        stE["acc"] = accs.next()
                        stE["acc1"] = acc1s.next()
                        stE["accp"] = accps.next()
                        stE["k"] = 0
                        acc0 = stE["acc"]
                        acc10 = stE["acc1"]
                        accp0 = stE["accp"]
                        mk.op("dve", lambda e, acc0=acc0: e.memset(acc0[:], 0.0), [], [acc0])
                        mk.op("dve", lambda e, acc10=acc10: e.memset(acc10[:], 0.0), [], [acc10])
                        mk.op("pool", lambda e, accp0=accp0: e.memset(accp0[:], 0.0), [], [accp0])
                    acc = stE["acc"]
                    acc1 = stE["acc1"]
                    accp = stE["accp"]
                    p = pts.next()
                    if kind == 0:
                        bsv = bs[:, :].rearrange("p (b c) -> p b c", b=2)[:, :, 0:W2]
                        if tiles[0] < 3:
                            for i, ki in enumerate(tiles):
                                if ki < 3:
                                    kb = cst_f[:, K_KB + ki:K_KB + ki + 1]
                                    mk.op("act", lambda e, p=p, bs=bs, i=i, kb=kb: e.activation(p[:, i, :], bs[:, i * 512:i * 512 + W2], AF.Exp, bias=kb, scale=SC), [bs, cst_f], [p])
                                else:
                                    mk.op("act", lambda e, p=p, bs=bs, i=i: e.activation(p[:, i, :], bs[:, i * 512:i * 512 + W2], AF.Exp, scale=SC), [bs], [p])
                        else:
                            mk.op("act", lambda e, p=p, bsv=bsv: e.activation(p[:], bsv, AF.Exp, scale=SC), [bs], [p])
                        if tiles[1] == ncom - 1:
                            mk.op("dve", lambda e, p=p: e.tensor_tensor(p[:, 1, 0:HT], p[:, 1, 0:HT], dmask_b[:], ALU.mult), [p, dmask_b], [p])
                        stE["k"] += 1
                        if stE["k"] % 3 == 0:
                            mk.op("pool", lambda e, p=p, accp=accp: e.tensor_tensor(accp[:], accp[:], p[:], ALU.add), [accp, p], [accp])
                        else:
                            mk.op("dve", lambda e, p=p, acc=acc: e.tensor_tensor(acc[:], acc[:], p[:], ALU.add), [acc, p], [acc])
                    else:
                        n = len(tiles)
                        pv = p[:].rearrange("p b c -> p (b c)")[:, 0:3 * HT].rearrange("p (t c) -> p t c", c=HT)
                        mk.op("act", lambda e, pv=pv, bs=bs, n=n: e.activation(pv[:, 0:n, :], bs[:, 0:n * HT].rearrange("p (t c) -> p t c", c=HT), AF.Exp, scale=SC), [bs], [p])
                        if last:
                            mk.op("dve", lambda e, pv=pv: e.tensor_tensor(pv[:, 0, :], pv[:, 0, :], dmask_b[:], ALU.mult), [p, dmask_b], [p])
                        mk.op("dve", lambda e, pv=pv, acc1=acc1, n=n: e.tensor_tensor(acc1[:, 0:n, :], acc1[:, 0:n, :], pv[:, 0:n, :], ALU.add), [acc1, p], [acc1])
                    if last:
                        ab = accb.next()
                        ab1 = acc1b.next()
                        mk.op("dve", lambda e, acc=acc, accp=accp: e.tensor_tensor(acc[:], acc[:], accp[:], ALU.add), [acc, accp], [acc])
                        mk.op("act", lambda e, ab=ab, acc=acc: e.activation(ab[:], acc[:], AF.Copy), [acc], [ab])
                        mk.op("act", lambda e, ab1=ab1, acc1=acc1: e.activation(ab1[:], acc1[:], AF.Copy), [acc1], [ab1])
                        tailq.append((ab, ab1))
                    return p

                def emit_pv(it, p):
                    u, kind, tiles, first, last = it
                    ncom = 8 * u + 4
                    nlast = ncom + 3
                    if first:
                        stP["bo"] = pbo2.next()
                    bo = stP["bo"]
                    if kind == 0:
                        for i, ki in enumerate(tiles):
                            mk.op("pe", lambda e, bo=bo, p=p, i=i, ki=ki: e.matmul(bo[:, 0:W2], vt[:, ki, :], p[:, i, :], start=(ki == 0), stop=False), [vt, p], [bo])
                    else:
                        pv = p[:].rearrange("p b c -> p (b c)")[:, 0:3 * HT].rearrange("p (t c) -> p t c", c=HT)
                        for i, ki in enumerate(tiles):
                            mk.op("pe", lambda e, bo=bo, pv=pv, i=i, ki=ki, nlast=nlast: e.matmul(bo[:, HT:W2], vt[:, ki, :], pv[:, i, :], start=False, stop=(ki == nlast)), [vt, p], [bo])
                    if last:
                        ab, ab1 = tailq.pop(0)
                        bd = pbd2.next()
                        for i in range(2):
                            mk.op("pe", lambda e, bd=bd, ab=ab, i=i: e.matmul(bd[:, 0:W2], ones_b[:], ab[:, i, :], start=(i == 0), stop=False), [ones_b, ab], [bd])
                        for i in range(3):
                            mk.op("pe", lambda e, bd=bd, ab1=ab1, i=i: e.matmul(bd[:, HT:W2], ones_b[:], ab1[:, i, :], start=False, stop=(i == 2)), [ones_b, ab1], [bd])
                        qs = slice(u * W2, (u + 1) * W2)
                        mk.op("dve", lambda e, bd=bd: e.tensor_scalar(rec[:], bd[:, 0:W2], 1e-30, None, ALU.max), [bd], [rec])
                        mk.op("dve", lambda e: e.reciprocal(rec[:], rec[:]), [rec], [rec])
                        mk.op("dve", lambda e, bo=bo, qs=qs: e.tensor_tensor(a_h[:, qs], bo[:, 0:W2], rec[:], ALU.mult), [bo, rec], [a_h])

                pend = []
                for it in items:
                    bs = emit_qk(it)
                    p = emit_exp(it, bs)
                    pend.append((it, p))
                    if len(pend) > 2:
                        emit_pv(*pend.pop(0))
                while pend:
                    emit_pv(*pend.pop(0))
                dma("pool", As[:, h, :], a_h[:], [a_h], As)
            mk.barrier()

        SCX = 512.0 ** -0.5
        G = 2 * HT
        NGC = NG // 2
        H2s = mk.dram("H2s", [128, KC, NO], F32)

        def make_helpers(pn, pj, wst, sqs):
            def ssq_rstd(src, src_ap, nchunk, width, out, inv_n, o0=0):
                for c in range(nchunk):
                    sq = sqs.next()
                    mk.op("act", lambda e, c=c, sq=sq: e.activation(sq[:, 0:width], src_ap(c), AF.Square), [src], [sq])
                    mk.op("pe", lambda e, c=c, sq=sq: e.matmul(pn[:, 0:width], ones_b[:], sq[:, 0:width], start=(c == 0), stop=(c == nchunk - 1)), [ones_b, sq], [pn])
                rstd_from_ps(pn, pn[:, 0:width], out, out[:, o0:o0 + width], out, out[:, o0:o0 + width], inv_n)

            def projw(Wb, oc, src, halves):
                wt = wst.next()
                dma("sp", wt[:], Wb[oc], [Wb], wt, nowaw=False)
                res = []
                for (c0, width) in halves:
                    bk = pj.next()
                    for kc in range(KC):
                        mk.op("pe", lambda e, kc=kc, bk=bk, c0=c0, width=width: e.matmul(bk[:, 0:width], wt[:, kc, :], src[:, kc, c0:c0 + width], start=(kc == 0), stop=(kc == KC - 1)), [wt, src], [bk])
                    res.append((bk, c0, width))
                return res
            return ssq_rstd, projw

        G1 = 4 * HT
        NG1 = NG // 4
        halves1 = [(0, 2 * HT), (2 * HT, 2 * HT)]
        with ExitStack() as es:
            mk.stack = es
            pj = Rot([mk.ps([128, 512], F32, f"pj{i}") for i in range(4)])
            pn = mk.ps([128, 512], F32, "pn")
            pa = Rot([mk.ps([128, 512], F32, f"pa{i}") for i in range(3)])
            hres = mk.sb([128, KC, G1], F32, "hres")
            y = mk.sb([128, KC, G1], F32, "y")
            xn = mk.sb([128, KC, G1], BF16, "xn")
            qx = mk.sb([128, KC, G1], BF16, "qx")
            sqs = Rot([mk.sb([128, G], BF16, f"sqs{i}") for i in range(3)])
            rs_a = mk.sb([128, G1], F32, "rs_a")
            rs_y = mk.sb([128, G1], F32, "rs_y")
            rs_h = mk.sb([128, G1], F32, "rs_h")
            recx = mk.sb([128, G], F32, "recx")
            pxs = Rot([mk.sb([128, G], BF16, f"px{i}") for i in range(4)])
            wst = Rot([mk.sb([128, KC, 128], BF16, f"wc{i}") for i in range(4)])
            kxT = mk.sb([128, KC, 256], BF16, "kxT")
            vx = mk.sb([128, 2, D], BF16, "vx")
            msq = mk.sb([128, KC, 256], BF16, "msq")
            wxv_t = mk.sb([128, KC, 512], BF16, "wxv_t")
            rcol = mk.sb([128, 2], F32, "rcol")
            ssq_rstd, projw = make_helpers(pn, pj, wst, sqs)

            def norm_residual(goff):
                for (h0, wd_) in halves1:
                    ssq_rstd(y, lambda c, h0=h0, wd_=wd_: y[:, c, h0:h0 + wd_], KC, wd_, rs_y, 1.0 / D, o0=h0)
                mk.op("dve", lambda e: e.tensor_tensor(y[:], y[:], gn[:, goff:goff + KC].unsqueeze(2).to_broadcast([128, KC, G1]), ALU.mult), [y, gn], [y])
                mk.op("dve", lambda e: e.tensor_tensor(y[:], y[:], rs_y[:].unsqueeze(1).to_broadcast([128, KC, G1]), ALU.mult), [y, rs_y], [y])
                mk.op("dve", lambda e: e.tensor_tensor(hres[:], hres[:], y[:], ALU.add), [y, hres], [hres])

            dma("sp", y[:, :, 0:256], memT[:].rearrange("(kc p) s -> p kc s", p=128), [memT], y, nowaw=False)
            mk.op("act", lambda e: e.activation(msq[:], y[:, :, 0:256], AF.Square), [y], [msq])
            for c in range(KC):
                mk.op("pe", lambda e, c=c: e.matmul(pn[:, 0:256], ones_b[:], msq[:, c, :], start=(c == 0), stop=(c == KC - 1)), [ones_b, msq], [pn])
            rstd_from_ps(pn, pn[:, 0:256], rs_y, rs_y[:, 0:256], rs_y, rs_y[:, 0:256], 1.0 / D)
            mk.op("dve", lambda e: e.tensor_tensor(xn[:, :, 0:256], y[:, :, 0:256], gn[:, G_MEM:G_MEM + KC].unsqueeze(2).to_broadcast([128, KC, 256]), ALU.mult), [y, gn], [xn])
            for oc in range(16):
                (bk, _, _), = projw(Wxk_b, oc, xn, [(0, 256)])
                mk.op("dve", lambda e, bk=bk, oc=oc: e.tensor_tensor(kxT[:, oc, :], bk[:, 0:256], rs_y[:, 0:256], ALU.mult), [bk, rs_y], [kxT])
            for ktm in range(2):
                for c in range(KC):
                    mk.op("pe", lambda e, c=c, ktm=ktm: e.matmul(pn[:, ktm:ktm + 1], msq[:, c, ktm * 128:(ktm + 1) * 128], ones_b[:, 0:1], start=(c == 0), stop=(c == KC - 1)), [msq, ones_b], [pn])
            rstd_from_ps(pn, pn[:, 0:2], rcol, rcol[:], rcol, rcol[:], 1.0 / D)
            for ob in range(4):
                dma("sp", wxv_t[:], Wxv_b[ob], [Wxv_b], wxv_t, nowaw=False)
                for ktm in range(2):
                    bk = pj.next()
                    for kc in range(KC):
                        mk.op("pe", lambda e, kc=kc, ktm=ktm, bk=bk: e.matmul(bk[:, :], xn[:, kc, ktm * 128:(ktm + 1) * 128], wxv_t[:, kc, :], start=(kc == 0), stop=(kc == KC - 1)), [xn, wxv_t], [bk])
                    mk.op("dve", lambda e, bk=bk, ktm=ktm, ob=ob: e.tensor_scalar(vx[:, ktm, ob * 512:(ob + 1) * 512], bk[:, :], rcol[:, ktm:ktm + 1], None, ALU.mult), [bk, rcol], [vx])

            xTv = xT[:].rearrange("(kc p) s -> p kc s", p=128)
            for gc in range(NG1):
                c0 = gc * G1
                for t2 in range(4):
                    m = 4 * gc + t2
                    for q in range(2):
                        dma("sp", hres[:, 8 * q:8 * q + 8, t2 * HT:(t2 + 1) * HT], xTv[:, 8 * q:8 * q + 8, 512 * m + 382:512 * m + 512], [xT], hres, nowaw=(t2 + q > 0))
                dma("sp", xn[:, 0:8, :], As[:, :, c0:c0 + G1], [As], xn, nowaw=False)
                dma("sp", xn[:, 8:16, :], Rs[:, :, c0:c0 + G1], [Rs], xn, nowaw=True)
                for (h0, wd_) in halves1:
                    ssq_rstd(xn, lambda c, h0=h0, wd_=wd_: xn[:, c, h0:h0 + wd_], 8, wd_, rs_a, 1.0 / 1024, o0=h0)
                mk.op("dve", lambda e: e.tensor_tensor(xn[:, 0:8, :], xn[:, 0:8, :], gn[:, G_MLAO:G_MLAO + 8].unsqueeze(2).to_broadcast([128, 8, G1]), ALU.mult), [xn, gn], [xn])
                mk.op("dve", lambda e: e.tensor_tensor(xn[:, 0:8, :], xn[:, 0:8, :], rs_a[:].unsqueeze(1).to_broadcast([128, 8, G1]), ALU.mult), [xn, rs_a], [xn])
                for oc in range(16):
                    for (bk, h0, wd_) in projw(Wout_b, oc, xn, halves1):
                        mk.op("act", lambda e, bk=bk, oc=oc, h0=h0, wd_=wd_: e.activation(y[:, oc, h0:h0 + wd_], bk[:, 0:wd_], AF.Copy), [bk], [y])
                norm_residual(G_MIXPOST)
                for (h0, wd_) in halves1:
                    ssq_rstd(hres, lambda c, h0=h0, wd_=wd_: hres[:, c, h0:h0 + wd_], KC, wd_, rs_h, 1.0 / D, o0=h0)
                mk.op("dve", lambda e: e.tensor_tensor(xn[:], hres[:], gn[:, G_XPRE:G_XPRE + KC].unsqueeze(2).to_broadcast([128, KC, G1]), ALU.mult), [hres, gn], [xn])
                for oc in range(16):
                    for (bk, h0, wd_) in projw(Wxq_b, oc, xn, halves1):
                        mk.op("dve", lambda e, bk=bk, oc=oc, h0=h0, wd_=wd_: e.tensor_tensor(qx[:, oc, h0:h0 + wd_], bk[:, 0:wd_], rs_h[:, h0:h0 + wd_], ALU.mult), [bk, rs_h], [qx])
                for hx in range(4):
                    for (h0, wd_) in halves1:
                        ps_ = []
                        for ktm in range(2):
                            bs = pa.next()
                            for dc in range(4):
                                mk.op("pe", lambda e, bs=bs, dc=dc, ktm=ktm, hx=hx, h0=h0, wd_=wd_: e.matmul(bs[:, 0:wd_], kxT[:, 4 * hx + dc, ktm * 128:(ktm + 1) * 128], qx[:, 4 * hx + dc, h0:h0 + wd_], start=(dc == 0), stop=(dc == 3)), [kxT, qx], [bs])
                            p = pxs.next()
                            mk.op("act", lambda e, p=p, bs=bs, wd_=wd_: e.activation(p[:, 0:wd_], bs[:, 0:wd_], AF.Exp, scale=SCX), [bs], [p])
                            ps_.append(p)
                        for ktm in range(2):
                            mk.op("pe", lambda e, ktm=ktm, p=ps_[ktm], wd_=wd_: e.matmul(pn[:, 0:wd_], ones_b[:], p[:, 0:wd_], start=(ktm == 0), stop=(ktm == 1)), [ones_b, ps_[ktm]], [pn])
                        mk.op("dve", lambda e, wd_=wd_: e.reciprocal(recx[:, 0:wd_], pn[:, 0:wd_]), [pn], [recx])
                        for dc in range(4):
                            bo = pa.next()
                            for ktm in range(2):
                                mk.op("pe", lambda e, bo=bo, ktm=ktm, dc=dc, hx=hx, p=ps_[ktm], wd_=wd_: e.matmul(bo[:, 0:wd_], vx[:, ktm, (4 * hx + dc) * 128:(4 * hx + dc + 1) * 128], p[:, 0:wd_], start=(ktm == 0), stop=(ktm == 1)), [vx, ps_[ktm]], [bo])
                            mk.op("dve", lambda e, bo=bo, dc=dc, hx=hx, h0=h0, wd_=wd_: e.tensor_tensor(xn[:, 4 * hx + dc, h0:h0 + wd_], bo[:, 0:wd_], recx[:, 0:wd_], ALU.mult), [bo, recx], [xn])
                for oc in range(16):
                    for (bk, h0, wd_) in projw(Wxo_b, oc, xn, halves1):
                        mk.op("act", lambda e, bk=bk, oc=oc, h0=h0, wd_=wd_: e.activation(y[:, oc, h0:h0 + wd_], bk[:, 0:wd_], AF.Copy), [bk], [y])
                norm_residual(G_XPOST)
                for q in range(2):
                    dma("pool", H2s[:, 8 * q:8 * q + 8, c0:c0 + G1], hres[:, 8 * q:8 * q + 8, :], [hres], H2s)
            mk.barrier()

        G2 = 4 * HT
        NG2 = NG // 4
        with ExitStack() as es:
            mk.stack = es
            pj = Rot([mk.ps([128, 512], F32, f"qj{i}") for i in range(4)])
            pn = mk.ps([128, 512], F32, "qn_")
            pd = Rot([mk.ps([128, 512], F32, f"qd{i}") for i in range(2)])
            hres = mk.sb([128, KC, G2], F32, "hres2")
            y = mk.sb([128, KC, 512], F32, "y2")
            xn = mk.sb([128, KC, G2], BF16, "xn2")
            sqs = Rot([mk.sb([128, G2], BF16, f"sqq{i}") for i in range(2)])
            rs_y = mk.sb([128, 512], F32, "rs_y2")
            rs_h = mk.sb([128, G2], F32, "rs_h2")
            actb = mk.sb([128, NFC, 512], BF16, "actb")
            ugs = Rot([mk.sb([128, G2], F32, f"ug{i}") for i in range(2)])
            uvs = Rot([mk.sb([128, G2], F32, f"uv{i}") for i in range(2)])
            cgs = Rot([mk.sb([128, 4, 128], F32, f"cg{i}") for i in range(2)])
            cvs = Rot([mk.sb([128, 4, 128], F32, f"cv{i}") for i in range(2)])
            wst = Rot([mk.sb([128, KC, 128], BF16, f"wu{i}") for i in range(3)])
            wdn = Rot([mk.sb([128, NFC, 128], BF16, f"wd{i}") for i in range(2)])
            ssq_rstd, projw = make_helpers(pn, pj, wst, sqs)
            outTv = outT[:].rearrange("(kc p) s -> p kc s", p=128)
            halves = [(0, 2 * HT), (2 * HT, 2 * HT)]
            for g2 in range(NG2):
                c0 = g2 * G2
                for q in range(4):
                    dma("sp", hres[:, 4 * q:4 * q + 4, :], H2s[:, 4 * q:4 * q + 4, c0:c0 + G2], [H2s], hres, nowaw=(q > 0))
                for (h0, wd_) in halves:
                    ssq_rstd(hres, lambda c, h0=h0, wd_=wd_: hres[:, c, h0:h0 + wd_], KC, wd_, rs_h, 1.0 / D, o0=h0)
                mk.op("dve", lambda e: e.tensor_tensor(xn[:], hres[:], gn[:, G_FFNPRE:G_FFNPRE + KC].unsqueeze(2).to_broadcast([128, KC, G2]), ALU.mult), [hres, gn], [xn])
                for c in range(NFC):
                    outs = []
                    for (ci, ubuf, cbuf) in ((c, ugs, cgs), (NFC + c, uvs, cvs)):
                        u = ubuf.next()
                        for (bk, h0, wd_) in projw(Wup_b, ci, xn, halves):
                            mk.op("dve", lambda e, bk=bk, u=u, h0=h0, wd_=wd_: e.tensor_tensor(u[:, h0:h0 + wd_], bk[:, 0:wd_], rs_h[:, h0:h0 + wd_], ALU.mult), [bk, rs_h], [u])
                        if g2 == 0:
                            mk.op("dve", lambda e, u=u: e.tensor_scalar(u[:, 0:2], u[:, 0:2], cst_f[:, K_H0:K_H0 + 1], None, ALU.mult), [u, cst_f], [u])
                        cv = cbuf.next()
                        uv3 = u[:].rearrange("p (t c) -> p t c", c=HT)
                        w0 = gn[:, G_CW + ci:G_CW + ci + 1]
                        w1 = gn[:, G_CW + 88 + ci:G_CW + 88 + ci + 1]
                        w2 = gn[:, G_CW + 176 + ci:G_CW + 176 + ci + 1]
                        bb = gn[:, G_CB + ci:G_CB + ci + 1]
                        mk.op("dve", lambda e, cv=cv, uv3=uv3, w2=w2, bb=bb: e.tensor_scalar(cv[:], uv3[:, :, 2:HT], w2, bb, ALU.mult, ALU.add), [u, gn], [cv])
                        mk.op("dve", lambda e, cv=cv, uv3=uv3, w1=w1: e.scalar_tensor_tensor(cv[:], uv3[:, :, 1:HT - 1], w1, cv[:], ALU.mult, ALU.add), [u, gn, cv], [cv])
                        mk.op("dve", lambda e, cv=cv, uv3=uv3, w0=w0: e.scalar_tensor_tensor(cv[:], uv3[:, :, 0:HT - 2], w0, cv[:], ALU.mult, ALU.add), [u, gn, cv], [cv])
                        outs.append(cv)
                    cg, cvv = outs
                    mk.op("act", lambda e, cg=cg: e.activation(cg[:], cg[:], AF.Gelu_apprx_tanh), [cg], [cg])
                    mk.op("dve", lambda e, cg=cg, cvv=cvv, c=c: e.tensor_tensor(actb[:, c, :].rearrange("p (t c) -> p t c", c=128), cg[:], cvv[:], ALU.mult), [cg, cvv], [actb])
                for oc in range(16):
                    wt = wdn.next()
                    for ch in range(2):
                        dma("sp", wt[:, 22 * ch:22 * ch + 22, :], Wdn_b[oc, :, 22 * ch:22 * ch + 22, :], [Wdn_b], wt, nowaw=(ch > 0))
                    bk = pd.next()
                    for c in range(NFC):
                        mk.op("pe", lambda e, bk=bk, wt=wt, c=c: e.matmul(bk[:, :], wt[:, c, :], actb[:, c, :], start=(c == 0), stop=(c == NFC - 1)), [wt, actb], [bk])
                    mk.op("act", lambda e, bk=bk, oc=oc: e.activation(y[:, oc, :], bk[:, :], AF.Copy), [bk], [y])
                hown = hres[:].rearrange("p k (t c) -> p k t c", c=HT)[:, :, :, 2:HT]
                ssq_rstd(y, lambda c: y[:, c, :], KC, 512, rs_y, 1.0 / D)
                yv = y[:, :, :]
                mk.op("dve", lambda e: e.tensor_tensor(yv, yv, gn[:, G_FFNPOST:G_FFNPOST + KC].unsqueeze(2).to_broadcast([128, KC, 512]), ALU.mult), [y, gn], [y])
                mk.op("dve", lambda e: e.tensor_tensor(yv, yv, rs_y[:].unsqueeze(1).to_broadcast([128, KC, 512]), ALU.mult), [y, rs_y], [y])
                mk.op("dve", lambda e: e.tensor_tensor(yv.rearrange("p k (t c) -> p k t c", c=128), hown, yv.rearrange("p k (t c) -> p k t c", c=128), ALU.add), [y, hres], [y])
                for q in range(4):
                    dma("pool", outTv[:, 4 * q:4 * q + 4, g2 * 512:(g2 + 1) * 512], y[:, 4 * q:4 * q + 4, :], [y], outT)
            mk._wait("sp", *outT.lw)
            mk._wait("pool", *outT.lw)
    return nc, mk


def host_inputs(inp, NG, core):
    S = NG * 512
    b, j = core // 4, core % 4
    pad = (3 - j) * 128
    x = np.asarray(inp["x"], dtype=np.float32)
    xT = np.zeros((D, S), np.float32)
    nreal = S - pad
    xT[:, pad:] = x[b, :nreal, :].T
    d = {"xT": xT, "memT": np.ascontiguousarray(np.asarray(inp["mem"], np.float32)[b].T)}
    pos = (np.arange(S) - pad).astype(np.float32)
    inv = (1.0 / (10000.0 ** (np.arange(0, 64, 2, dtype=np.float32) / np.float32(64)))).astype(np.float32)
    ang = pos[None, :] * inv[:, None]
    cs = np.zeros((2, 64, S), np.float32)
    cs[0, :32] = np.cos(ang); cs[0, 32:] = np.cos(ang)
    cs[1, :32] = np.sin(ang); cs[1, 32:] = np.sin(ang)
    d["cs"] = cs
    cst = np.zeros((128, NCST), np.float32)
    cst[:, K_ID:K_ID + 128] = np.eye(128)
    cm = np.ones(512, np.float32); cm[::64] = 0
    cst[:, K_CM:K_CM + 512] = cm[None, :]
    s_i = np.arange(128)[:, None]; t_i = np.arange(128)[None, :]
    cst[:, K_TRI:K_TRI + 128] = ((s_i // 64 == t_i // 64) & (s_i <= t_i)).astype(np.float32)
    qc = np.concatenate([[-1, -1], np.arange(128) // 64])[None, :]
    cst[:, K_DM:K_DM + HT] = ((s_i // 64) <= qc).astype(np.float32)
    for t in range(3):
        cst[:, K_KB + t] = -30000.0 if t < 3 - j else 0.0
    cst[:, K_H0] = 0.0 if j == 0 else 1.0
    d["cst"] = cst
    gn = np.zeros((128, NGN), np.float32)

    def pk(v):
        v = np.asarray(v, np.float32).reshape(-1)
        return v.reshape(-1, 128).T

    for name, off in (("ln_mix_pre", G_MIXPRE), ("ln_mix_post", G_MIXPOST), ("ln_x_pre", G_XPRE), ("ln_x_post", G_XPOST),
                      ("mem_norm", G_MEM), ("ln_ffn_pre", G_FFNPRE), ("ln_ffn_post", G_FFNPOST), ("q_norm", G_QN),
                      ("kv_norm", G_KVN), ("mla_out_norm", G_MLAO), ("hgrn_out_norm", G_HGO)):
        a = pk(inp[name][0])
        gn[:, off:off + a.shape[1]] = a
    lbv = np.asarray(inp["hgrn_lb"], np.float32)
    gn[:, G_LB0:G_LB0 + 8] = pk(lbv[0]); gn[:, G_LB1:G_LB1 + 8] = pk(lbv[1])
    cw = np.asarray(inp["conv_w"], np.float32)[0]
    for k in range(3):
        gn[:, G_CW + 88 * k:G_CW + 88 * (k + 1)] = pk(cw[k])
    gn[:, G_CB:G_CB + 88] = pk(inp["conv_b"][0])
    d["gains"] = gn
    for name in ("w_in", "w_uq", "w_ukv", "w_out", "w_xq", "w_xk", "w_xv", "w_xo", "w_up", "w_down"):
        d[name] = np.ascontiguousarray(np.asarray(inp[name], np.float32)[0])
    return d


_CACHE = {}


def kernel(**inputs):
    NG = 32
    if NG not in _CACHE:
        _CACHE[NG] = build(NG)[0]
    nc = _CACHE[NG]
    in_maps = [host_inputs(inputs, NG, c) for c in range(8)]
    res = run_bass_kernel_spmd(nc, in_maps, core_ids=list(range(8)))
    out = np.zeros((2, 16384, D), np.float32)
    for c in range(8):
        b, j = c // 4, c % 4
        o = res.results[c]["outT"]
        o = o.reshape(D, NG, 128)
        for m in range(NG):
            i = 4 * m + j
            out[b, i * 128:(i + 1) * 128, :] = o[:, m, :].T
    return out
```

```python
import numpy as np
import concourse.bass as bass
import concourse.mybir as mybir
from concourse.bass_utils import run_bass_kernel_spmd
from contextlib import ExitStack

F32 = mybir.dt.float32
BF16 = mybir.dt.bfloat16
AF = mybir.ActivationFunctionType
ALU = mybir.AluOpType

D = 2048
KC = 16
DFF = 5632
NFC = 44
EPS = 1e-6
HT = 130
C_Q, C_KV, C_KR, C_HQ, C_HF, C_HI, C_HG = 0, 512, 768, 832, 1856, 2880, 3904

G_MIXPRE, G_MIXPOST, G_XPRE, G_XPOST, G_MEM, G_FFNPRE, G_FFNPOST = 0, 16, 32, 48, 64, 80, 96
G_QN, G_KVN, G_MLAO, G_HGO, G_LB0, G_LB1 = 112, 116, 118, 126, 127, 135
G_CW, G_CB = 143, 143 + 264
NGN = 143 + 264 + 88
K_ID, K_CM, K_TRI, K_DM, K_KB, K_H0 = 0, 128, 640, 768, 898, 902
NCST = 903


class T:
    __slots__ = ("ap", "name", "lw", "rd", "dsem", "dcnt")

    def __init__(self, ap, name):
        self.ap = ap
        self.name = name
        self.lw = None
        self.rd = {}
        self.dsem = None
        self.dcnt = 0

    def __getitem__(self, idx):
        return self.ap[idx]


class MK:
    ROLL = 16000

    def __init__(self, nc):
        self.nc = nc
        self.eng = {"pe": nc.tensor, "act": nc.scalar, "dve": nc.vector, "pool": nc.gpsimd, "sp": nc.sync}
        self.sem = {k: nc.alloc_semaphore(f"s_{k}") for k in self.eng}
        self.allsem = {k: [self.sem[k]] for k in self.eng}
        self.cnt = {k: 0 for k in self.eng}
        self.seen = {k: {} for k in self.eng}
        self.ninst = {k: 0 for k in self.eng}
        self.nwait = {k: 0 for k in self.eng}
        self.dtiles = []
        self._uid = 0
        self.stack = None

    def sb(self, shape, dtype, name="t"):
        self._uid += 1
        name = f"{name}_{self._uid}"
        h = self.stack.enter_context(self.nc.sbuf_tensor(name, list(shape), dtype))
        return T(h.ap(), name)

    def ps(self, shape, dtype=F32, name="p"):
        self._uid += 1
        name = f"{name}_{self._uid}"
        h = self.stack.enter_context(self.nc.psum_tensor(name, list(shape), dtype))
        return T(h.ap(), name)

    def dram(self, name, shape, dtype, kind="Internal"):
        return T(self.nc.dram_tensor(name, list(shape), dtype, kind=kind).ap(), name)

    def _wait(self, e, sem, val):
        seen = self.seen[e]
        if seen.get(sem, 0) >= val:
            return
        self.eng[e].wait_ge(sem, val)
        self.nwait[e] += 1
        seen[sem] = val

    def op(self, e, fn, reads=(), writes=(), dma=False, nowaw=False):
        deps = {}

        def add(d):
            if d is None:
                return
            s, v = d
            if deps.get(s, 0) < v:
                deps[s] = v

        for t in reads:
            add(t.lw)
        for t in writes:
            if not nowaw:
                add(t.lw)
            for s, v in t.rd.items():
                add((s, v))
        for s, v in deps.items():
            if e == "pe" and any(s is o for o in self.allsem["pe"]):
                continue
            self._wait(e, s, v)
        ins = fn(self.eng[e])
        self.ninst[e] += 1
        if dma:
            t = writes[0]
            if t.dsem is None:
                t.dsem = self.nc.alloc_semaphore(f"d_{t.name}")
                self.dtiles.append(t)
            t.dcnt += 16
            ins.then_inc(t.dsem, 16)
            tok = (t.dsem, t.dcnt)
        else:
            if self.cnt[e] >= self.ROLL:
                self.sem[e] = self.nc.alloc_semaphore(f"s_{e}_{self.ninst[e]}")
                self.allsem[e].append(self.sem[e])
                self.cnt[e] = 0
            self.cnt[e] += 1
            ins.then_inc(self.sem[e], 1)
            tok = (self.sem[e], self.cnt[e])
        for t in writes:
            t.lw = tok
            t.rd = {}
        s, v = tok
        for t in reads:
            if t.rd.get(s, 0) < v:
                t.rd[s] = v
        return tok

    def barrier(self):
        toks = [(self.sem[k], self.cnt[k]) for k in self.eng if self.cnt[k] > 0]
        toks += [(t.dsem, t.dcnt) for t in self.dtiles]
        for e in self.eng:
            for s, v in toks:
                self._wait(e, s, v)

    def release_dsems(self, tiles):
        for t in tiles:
            if t.dsem is not None:
                self.dtiles.remove(t)
                t.dsem = None


class Rot:
    def __init__(self, items):
        self.items = items
        self.i = 0

    def next(self):
        x = self.items[self.i % len(self.items)]
        self.i += 1
        return x


def build(NG, dbg=False):
    S = NG * 512
    NO = NG * HT
    NT = NG * 4
    nc = bass.Bass("TRN2", target_bir_lowering=False)
    mk = MK(nc)
    I = "ExternalInput"
    xT = mk.dram("xT", [D, S], F32, I)
    memT = mk.dram("memT", [D, 256], F32, I)
    cs = mk.dram("cs", [2, 64, S], F32, I)
    cst = mk.dram("cst", [128, NCST], F32, I)
    gains = mk.dram("gains", [128, NGN], F32, I)
    w_in = mk.dram("w_in", [D, 4928], F32, I)
    w_uq = mk.dram("w_uq", [512, 1536], F32, I)
    w_ukv = mk.dram("w_ukv", [256, 2048], F32, I)
    w_out = mk.dram("w_out", [D, D], F32, I)
    w_xq = mk.dram("w_xq", [D, D], F32, I)
    w_xk = mk.dram("w_xk", [D, D], F32, I)
    w_xv = mk.dram("w_xv", [D, D], F32, I)
    w_xo = mk.dram("w_xo", [D, D], F32, I)
    w_up = mk.dram("w_up", [D, 2 * DFF], F32, I)
    w_down = mk.dram("w_down", [DFF, D], F32, I)
    outT = mk.dram("outT", [D, NG * 128], F32, "ExternalOutput")
    SK = "ExternalOutput" if dbg else "Internal"
    in_cols = ([C_KV, C_KV + 128] + [C_HF + 128 * i for i in range(8)] + [C_HI + 128 * i for i in range(8)]
               + [C_Q + 128 * i for i in range(4)] + [C_HQ + 128 * i for i in range(8)] + [C_HG + 128 * i for i in range(8)])
    NIC = len(in_cols)
    Win_b = mk.dram("Win_b", [NIC, 128, KC, 128], BF16)
    Wout_b = mk.dram("Wout_b", [16, 128, KC, 128], BF16)
    Wxq_b = mk.dram("Wxq_b", [16, 128, KC, 128], BF16)
    Wxk_b = mk.dram("Wxk_b", [16, 128, KC, 128], BF16)
    Wxv_b = mk.dram("Wxv_b", [4, 128, KC, 512], BF16)
    Wxo_b = mk.dram("Wxo_b", [16, 128, KC, 128], BF16)
    Wup_b = mk.dram("Wup_b", [88, 128, KC, 128], BF16)
    Wdn_b = mk.dram("Wdn_b", [16, 128, NFC, 128], BF16)
    KTs = mk.dram("KTs", [8, 128, S], BF16, SK)
    KRs = mk.dram("KRs", [64, S], BF16, SK)
    VTs = mk.dram("VTs", [8, 128, NT, 128], BF16, SK)
    QNs = mk.dram("QNs", [8, 128, NO], BF16, SK)
    QRs = mk.dram("QRs", [8, 64, NO], BF16, SK)
    Rs = mk.dram("Rs", [128, 8, NO], BF16, SK)
    As = mk.dram("As", [128, 8, NO], BF16, SK)

    def dma(q, out_ap, in_ap, reads, wt, nowaw=True):
        mk.op(q, lambda e: e.dma_start(out=out_ap, in_=in_ap), reads=reads, writes=[wt], dma=True, nowaw=nowaw)

    def conv_w(W, Wb, cols, kc):
        for oc, c0 in enumerate(cols):
            dma("pool", Wb[oc], W[:, c0:c0 + 128].rearrange("(kc p) n -> p kc n", p=128), [W], Wb)

    conv_w(w_in, Win_b, in_cols, KC)
    conv_jobs = []
    for W, Wb in ((w_out, Wout_b), (w_xq, Wxq_b), (w_xk, Wxk_b), (w_xo, Wxo_b)):
        for oc in range(16):
            conv_jobs.append((Wb, Wb[oc], W, W[:, oc * 128:(oc + 1) * 128].rearrange("(kc p) n -> p kc n", p=128)))
    for ob in range(4):
        for kh in range(2):
            conv_jobs.append((Wxv_b, Wxv_b[ob, :, 8 * kh:8 * kh + 8, :], w_xv,
                              w_xv[1024 * kh:1024 * (kh + 1), ob * 512:(ob + 1) * 512].rearrange("(kc p) n -> p kc n", p=128)))
    for oc in range(88):
        conv_jobs.append((Wup_b, Wup_b[oc], w_up, w_up[:, oc * 128:(oc + 1) * 128].rearrange("(kc p) n -> p kc n", p=128)))
    for oc in range(16):
        for ch in range(2):
            conv_jobs.append((Wdn_b, Wdn_b[oc, :, 22 * ch:22 * ch + 22, :], w_down,
                              w_down[2816 * ch:2816 * (ch + 1), oc * 128:(oc + 1) * 128].rearrange("(kc p) n -> p kc n", p=128)))
    per_g = -(-len(conv_jobs) // NG)

    def emit_conv(n):
        for _ in range(n):
            if conv_jobs:
                Wb, o_ap, W, i_ap = conv_jobs.pop(0)
                dma("pool", o_ap, i_ap, [W], Wb)

    with ExitStack() as es_all:
        mk.stack = es_all
        cst_f = mk.sb([128, NCST], F32, "cst_f")
        gn = mk.sb([128, NGN], F32, "gn")
        dma("sp", cst_f[:], cst[:], [cst], cst_f)
        dma("sp", gn[:], gains[:], [gains], gn)
        ident_b = mk.sb([128, 128], BF16, "ident_b")
        ones_b = mk.sb([128, 128], BF16, "ones_b")
        tri_b = mk.sb([128, 128], BF16, "tri_b")
        dmask_b = mk.sb([128, HT], BF16, "dmask_b")
        mk.op("dve", lambda e: e.tensor_copy(ident_b[:], cst_f[:, K_ID:K_ID + 128]), [cst_f], [ident_b])
        mk.op("dve", lambda e: e.memset(ones_b[:], 1.0), [], [ones_b])
        mk.op("dve", lambda e: e.tensor_copy(tri_b[:], cst_f[:, K_TRI:K_TRI + 128]), [cst_f], [tri_b])
        mk.op("dve", lambda e: e.tensor_copy(dmask_b[:], cst_f[:, K_DM:K_DM + HT]), [cst_f], [dmask_b])
        lb = mk.sb([128, 8], F32, "lb")
        oml = mk.sb([128, 8], F32, "oml")
        mk.op("dve", lambda e: e.tensor_tensor(lb[:], gn[:, G_LB0:G_LB0 + 8], gn[:, G_LB1:G_LB1 + 8], ALU.subtract), [gn], [lb])
        mk.op("act", lambda e: e.activation(lb[:], lb[:], AF.Sigmoid), [lb], [lb])
        mk.op("dve", lambda e: e.tensor_scalar(oml[:], lb[:], -1.0, 1.0, ALU.mult, ALU.add), [lb], [oml])

        def rstd_from_ps(ps_t, ps_ap, out_t, out_ap, tmp_t, tmp_ap, inv_n):
            mk.op("dve", lambda e: e.tensor_scalar(tmp_ap, ps_ap, inv_n, EPS, ALU.mult, ALU.add), [ps_t], [tmp_t])
            mk.op("act", lambda e: e.activation(tmp_ap, tmp_ap, AF.Sqrt), [tmp_t], [tmp_t])
            mk.op("dve", lambda e: e.reciprocal(out_ap, tmp_ap), [tmp_t], [out_t])

        with ExitStack() as es:
            mk.stack = es
            A_tiles = []
            banks = [mk.ps([128, 512], F32, f"bk{i}") for i in range(7)]
            ps_tr = mk.ps([128, 1024], BF16, "ps_tr")
            pproj = Rot(banks[0:2])
            ps_n = banks[2]
            pst = Rot(banks[3:5])
            pbo2 = Rot(banks[5:7])
            wkr = mk.sb([128, KC, 64], BF16, "wkr")
            wkrr = mk.sb([128, KC, 64], BF16, "wkrr")
            dma("pool", wkr[:], w_in[:, C_KR:C_KR + 64].rearrange("(kc p) n -> p kc n", p=128), [w_in], wkr)
            mk.op("dve", lambda e: e.tensor_scalar(wkrr[:, :, 0:32], wkr[:, :, 32:64], -1.0, None, ALU.mult), [wkr], [wkrr])
            mk.op("dve", lambda e: e.tensor_copy(wkrr[:, :, 32:64], wkr[:, :, 0:32]), [wkr], [wkrr])
            wuq = mk.sb([128, 4, 1536], BF16, "wuq")
            dma("pool", wuq[:], w_uq[:].rearrange("(kc p) n -> p kc n", p=128), [w_uq], wuq)
            wuqr = mk.sb([128, 4, 8, 64], BF16, "wuqr")
            wuq_v = wuq[:].rearrange("p k (h d) -> p k h d", d=192)
            mk.op("dve", lambda e: e.tensor_scalar(wuqr[:, :, :, 0:32], wuq_v[:, :, :, 160:192], -1.0, None, ALU.mult), [wuq], [wuqr])
            mk.op("dve", lambda e: e.tensor_copy(wuqr[:, :, :, 32:64], wuq_v[:, :, :, 128:160]), [wuq], [wuqr])
            wukv = mk.sb([128, 2, 2048], BF16, "wukv")
            dma("pool", wukv[:], w_ukv[:].rearrange("(kc p) n -> p kc n", p=128), [w_ukv], wukv)
            wukv_v = wukv[:].rearrange("p k (h t d) -> p k h t d", t=2, d=128)
            xq = Rot([mk.sb([128, 2, 512], F32, f"xq{i}") for i in range(2)])
            sqq = Rot([mk.sb([128, 2, 512], BF16, f"sq{i}") for i in range(2)])
            xgs = Rot([mk.sb([128, KC, 512], BF16, f"xg{i}") for i in range(2)])
            wst = Rot([mk.sb([128, KC, 128], BF16, f"wst{i}") for i in range(3)])
            rstd = mk.sb([128, 512], F32, "rstd")
            ckv = mk.sb([128, 2, 512], F32, "ckv")
            sqtmp = mk.sb([128, 8 * HT], BF16, "sqtmp")
            ckvsq_v = sqtmp[:, 0:1024].rearrange("p (c s) -> p c s", c=2)
            o_sq_v = sqtmp[:, :].rearrange("p (h t) -> p h t", t=HT)
            cq_sq_v = sqtmp[:, 0:4 * HT].rearrange("p (h t) -> p h t", t=HT)
            rkv = mk.sb([128, 512], F32, "rkv")
            tmpn = rkv
            ckvn = mk.sb([128, 2, 512], BF16, "ckvn")
            kt_sb = mk.sb([128, 4, 512], BF16, "kt_sb")
            vm_sb = mk.sb([128, 4, 4, 128], BF16, "vm_sb")
            kr_f = mk.sb([64, 512], F32, "kr_f")
            krr_f = mk.sb([64, 512], F32, "krr_f")
            cos_t = mk.sb([64, 512], F32, "cos_t")
            sin_t = mk.sb([64, 512], F32, "sin_t")
            kro = mk.sb([64, 512], BF16, "kro")
            Fh = mk.sb([128, 4, 512], F32, "Fh")
            Lh = mk.sb([128, 4, 512], F32, "Lh")
            Bh = mk.sb([128, 4, 512], F32, "Bh")
            ebl = mk.sb([128, 4, 8], F32, "ebl")
            kdl = mk.sb([128, 4, 512], BF16, "kdl")
            vT = mk.sb([128, 4, 512], BF16, "vT")
            kdl_tok = mk.sb([128, 4, 4, 128], BF16, "kdl_tok")
            v_tok = mk.sb([128, 4, 4, 128], BF16, "v_tok")
            hq_f = mk.sb([128, 4, 256], F32, "hq_f")
            qd = mk.sb([128, 4, 256], BF16, "qd")
            kdo = mk.sb([128, 4, 256], BF16, "kdo")
            at_sb = Rot([mk.sb([128, 128], BF16, f"at{i}") for i in range(2)])
            st_f = [mk.sb([128, 128], F32, f"stf{h}") for h in range(8)]
            st_snap = [mk.sb([128, 4, 128], BF16, f"stb{h}") for h in range(8)]
            o_f = mk.sb([128, 8, HT], F32, "o_f")
            ro = mk.sb([128, 8, HT], F32, "ro")
            hg_f = mk.sb([128, 8, HT], F32, "hg_f")
            r_b = mk.sb([128, 8, HT], BF16, "r_b")
            cq_f = mk.sb([128, 4, HT], F32, "cq_f")
            rq = mk.sb([128, HT], F32, "rq")
            cqn = mk.sb([128, 4, HT], BF16, "cqn")
            qn_sb = mk.sb([128, 8, HT], BF16, "qn_sb")
            qr_f = mk.sb([64, 3, HT], F32, "qr_f")
            qrr_f = mk.sb([64, 3, HT], F32, "qrr_f")
            qr_sb = mk.sb([64, 8, HT], BF16, "qr_sb")
            for h in range(8):
                mk.op("dve", lambda e, h=h: e.memset(st_f[h][:], 0.0), [], [st_f[h]])
            cmask = cst_f[:, K_CM:K_CM + 512]
            wi = [0]

            def load_w(ci):
                t = wst.next()
                dma("sp", t[:], Win_b[ci], [Win_b], t, nowaw=False)
                return t

            def proj(ci, xg, c0, c1, M=128, wt=None, wap=None):
                if wt is None:
                    wt = load_w(ci)
                    wap = lambda kc: wt[:, kc, :]
                bk = pproj.next()
                n = c1 - c0
                for kc in range(KC):
                    mk.op("pe", lambda e, kc=kc: e.matmul(bk[0:M, 0:n], wap(kc), xg[:, kc, c0:c1], start=(kc == 0), stop=(kc == KC - 1)),
                          [wt, xg], [bk])
                return bk, bk[0:M, 0:n]

            accx = mk.sb([128, 512], F32, "accx")
            accxb = sqtmp[:, 0:512]
            xg_of = {}

            xTv_a = xT[:].rearrange("(kc p) s -> p kc s", p=128)
            xbuf = xq.items
            xstate = {"n": 0}
            NCH = NG * 8

            def x_issue(n):
                if n >= NCH:
                    return
                g, q = divmod(n, 8)
                xt = xbuf[n % 2]
                dma("pool", xt[:], xTv_a[:, 2 * q:2 * q + 2, g * 512:g * 512 + 512], [xT], xt, nowaw=False)

            def x_consume(n):
                g, q = divmod(n, 8)
                if q == 0:
                    xg_of[g] = xgs.next()
                xg = xg_of[g]
                xt = xbuf[n % 2]
                sq = sqq.next()
                mk.op("act", lambda e, xt=xt, sq=sq: e.activation(sq[:], xt[:], AF.Square), [xt], [sq])
                for k in range(2):
                    kc = 2 * q + k
                    if kc == 0:
                        mk.op("pool", lambda e, sq=sq, k=k: e.tensor_copy(accx[:], sq[:, k, :]), [sq], [accx])
                    else:
                        mk.op("pool", lambda e, sq=sq, k=k: e.tensor_tensor(accx[:], accx[:], sq[:, k, :], ALU.add), [sq, accx], [accx])
                    mk.op("dve", lambda e, xt=xt, k=k, kc=kc, xg=xg: e.tensor_scalar(xg[:, kc, :], xt[:, k, :], gn[:, G_MIXPRE + kc:G_MIXPRE + kc + 1], None, ALU.mult),
                          [xt, gn], [xg])

            def xstep():
                n = xstate["n"]
                if n >= NCH:
                    return
                x_consume(n)
                x_issue(n + 2)
                xstate["n"] = n + 1

            class Filler:
                def __init__(self, items, k=1):
                    self.items = list(items)
                    self.pending = []
                    self.k = k

                def step(self):
                    for ev in self.pending:
                        ev()
                    self.pending = []
                    for _ in range(self.k):
                        if self.items:
                            pe_fn, ev_fn = self.items.pop(0)
                            ctx = pe_fn()
                            self.pending.append(lambda ctx=ctx, ev_fn=ev_fn: ev_fn(ctx))

                def flush(self):
                    while self.items or self.pending:
                        self.step()

            x_issue(0)
            x_issue(1)
            for _ in range(8):
                xstep()
            for g in range(NG):
                s0 = g * 512
                xg = xg_of[g]
                mk.op("act", lambda e: e.activation(accxb, accx[:], AF.Copy), [accx], [sqtmp])
                mk.op("pe", lambda e: e.matmul(ps_n[:, :], ones_b[:], accxb, start=True, stop=True), [sqtmp, ones_b], [ps_n])
                rstd_from_ps(ps_n, ps_n[:, :], rstd, rstd[:], rstd, rstd[:], 1.0 / D)
                dma("sp", cos_t[:], cs[0, :, s0:s0 + 512], [cs], cos_t, nowaw=False)
                dma("sp", sin_t[:], cs[1, :, s0:s0 + 512], [cs], sin_t, nowaw=False)
                for c in range(2):
                    bk, ap = proj(c, xg, 0, 512)
                    mk.op("dve", lambda e, ap=ap, c=c: e.tensor_tensor(ckv[:, c, :], ap, rstd[:], ALU.mult), [bk, rstd], [ckv])
                bk, ap = proj(None, xg, 0, 512, M=64, wt=wkr, wap=lambda kc: wkr[:, kc, :])
                mk.op("dve", lambda e, ap=ap: e.tensor_tensor(kr_f[:], ap, rstd[0:64, :], ALU.mult), [bk, rstd], [kr_f])
                bk, ap = proj(None, xg, 0, 512, M=64, wt=wkrr, wap=lambda kc: wkrr[:, kc, :])
                mk.op("dve", lambda e, ap=ap: e.tensor_tensor(krr_f[:], ap, rstd[0:64, :], ALU.mult), [bk, rstd], [krr_f])
                mk.op("dve", lambda e: e.tensor_tensor(kr_f[:], kr_f[:], cos_t[:], ALU.mult), [kr_f, cos_t], [kr_f])
                mk.op("dve", lambda e: e.tensor_tensor(krr_f[:], krr_f[:], sin_t[:], ALU.mult), [krr_f, sin_t], [krr_f])
                mk.op("dve", lambda e: e.tensor_tensor(kro[:], kr_f[:], krr_f[:], ALU.add), [kr_f, krr_f], [kro])
                dma("pool", KRs[:, s0:s0 + 512], kro[:], [kro], KRs)
                mk.op("act", lambda e: e.activation(ckvsq_v, ckv[:], AF.Square), [ckv], [sqtmp])
                for c in range(2):
                    mk.op("pe", lambda e, c=c: e.matmul(ps_n[:, :], ones_b[:], ckvsq_v[:, c, :], start=(c == 0), stop=(c == 1)), [sqtmp, ones_b], [ps_n])
                rstd_from_ps(ps_n, ps_n[:, :], rkv, rkv[:], rkv, rkv[:], 1.0 / 256)
                for c in range(2):
                    mk.op("dve", lambda e, c=c: e.scalar_tensor_tensor(ckvn[:, c, :], ckv[:, c, :], gn[:, G_KVN + c:G_KVN + c + 1], rkv[:], ALU.mult, ALU.mult),
                          [ckv, gn, rkv], [ckvn])
                def mk_k(hh2, i):
                    h = 4 * hh2 + i

                    def pe_fn():
                        bk = pproj.next()
                        for c in range(2):
                            mk.op("pe", lambda e, c=c, bk=bk: e.matmul(bk[:, :], wukv_v[:, c, h, 0, :], ckvn[:, c, :], start=(c == 0), stop=(c == 1)), [wukv, ckvn], [bk])
                        return bk

                    def ev_fn(bk):
                        mk.op("act", lambda e: e.activation(kt_sb[:, i, :], bk[:, :], AF.Copy), [bk], [kt_sb])
                        if i == 3:
                            dma("pool", KTs[4 * hh2:4 * hh2 + 4, :, s0:s0 + 512].rearrange("h p s -> p h s"), kt_sb[:], [kt_sb], KTs)
                    return pe_fn, ev_fn

                def mk_v(hh2, tt):
                    def pe_fn():
                        bk = pproj.next()
                        for c in range(2):
                            mk.op("pe", lambda e, c=c, bk=bk: e.matmul(bk[:, :].rearrange("p (h d) -> p h d", d=128), ckvn[:, c, tt * 128:(tt + 1) * 128],
                                                                     wukv_v[:, c, 4 * hh2:4 * hh2 + 4, 1, :], start=(c == 0), stop=(c == 1)), [wukv, ckvn], [bk])
                        return bk

                    def ev_fn(bk):
                        mk.op("act", lambda e: e.activation(vm_sb[:, :, tt, :], bk[:, :].rearrange("p (h d) -> p h d", d=128), AF.Copy), [bk], [vm_sb])
                        if tt == 3:
                            dma("pool", VTs[4 * hh2:4 * hh2 + 4, :, 4 * g:4 * g + 4, :].rearrange("h p t d -> p h t d"), vm_sb[:], [vm_sb], VTs)
                    return pe_fn, ev_fn

                fillkv = []
                for hh2 in range(2):
                    fillkv += [mk_k(hh2, i) for i in range(4)] + [mk_v(hh2, tt) for tt in range(4)]

                def mk_own_proj(ci, dst, idx):
                    def pe_fn():
                        return proj(ci, xg, 382, 512)

                    def ev_fn(ctx):
                        bk, ap = ctx
                        mk.op("dve", lambda e: e.tensor_tensor(dst[:, idx, :], ap, rstd[:, 382:512], ALU.mult), [bk, rstd], [dst])
                    return pe_fn, ev_fn

                fill0 = [mk_own_proj(30 + h, hg_f, h) for h in range(8)] + [mk_own_proj(18 + c, cq_f, c) for c in range(4)]

                def mk_qn(hs):
                    n = len(hs)

                    def pe_fn():
                        bk = pproj.next()
                        for k, h in enumerate(hs):
                            for c in range(4):
                                mk.op("pe", lambda e, k=k, h=h, c=c, bk=bk: e.matmul(bk[:, k * HT:(k + 1) * HT], wuq[:, c, h * 192:h * 192 + 128], cqn[:, c, :], start=(c == 0), stop=(c == 3)), [wuq, cqn], [bk])
                        return bk

                    def ev_fn(bk):
                        mk.op("act", lambda e: e.activation(qn_sb[:, hs[0]:hs[0] + n, :], bk[:, 0:n * HT].rearrange("p (h t) -> p h t", t=HT), AF.Copy), [bk], [qn_sb])
                    return pe_fn, ev_fn

                def mk_qr(hs, wsel):
                    n = len(hs)
                    dst = qr_f if wsel == 0 else qrr_f
                    cs_t = cos_t if wsel == 0 else sin_t

                    def pe_fn():
                        bk = pproj.next()
                        for k, h in enumerate(hs):
                            for c in range(4):
                                lw = (wuq[:, c, h * 192 + 128:h * 192 + 192] if wsel == 0 else wuqr[:, c, h, :])
                                mk.op("pe", lambda e, k=k, c=c, bk=bk, lw=lw: e.matmul(bk[0:64, k * HT:(k + 1) * HT], lw, cqn[:, c, :], start=(c == 0), stop=(c == 3)), [wuq, wuqr, cqn], [bk])
                        return bk

                    def ev_fn(bk):
                        mk.op("dve", lambda e: e.tensor_tensor(dst[:, 0:n, :], bk[0:64, 0:n * HT].rearrange("p (h t) -> p h t", t=HT),
                                                               cs_t[:, 382:512].unsqueeze(1).to_broadcast([64, n, HT]), ALU.mult), [bk, cs_t], [dst])
                        if wsel == 1:
                            mk.op("dve", lambda e: e.tensor_tensor(qr_sb[:, hs[0]:hs[0] + n, :], qr_f[:, 0:n, :], qrr_f[:, 0:n, :], ALU.add), [qr_f, qrr_f], [qr_sb])
                    return pe_fn, ev_fn

                fill1 = []
                for hb in range(3):
                    hs = list(range(3 * hb, min(3 * hb + 3, 8)))
                    fill1 += [mk_qn(hs), mk_qr(hs, 0), mk_qr(hs, 1)]

                def mk_hproj(hh):
                    items = []
                    for (cb, dst, c0) in ((2, Fh, 0), (10, vT, 0), (22, hq_f, 256)):
                        for i in range(4):
                            def pe_fn(cb=cb, i=i, c0=c0):
                                return proj(cb + 4 * hh + i, xg, c0, 512)

                            def ev_fn(ctx, dst=dst, i=i, c0=c0):
                                bk, ap = ctx
                                mk.op("dve", lambda e: e.tensor_tensor(dst[:, i, :], ap, rstd[:, c0:512], ALU.mult), [bk, rstd], [dst])
                                if c0 == 0 and i % 2 == 0:
                                    xstep()
                            items.append((pe_fn, ev_fn))
                    return items

                for hh in range(2):
                    fl = Filler(fillkv + fill0, 2) if hh == 0 else Filler(fill1, 1)
                    if hh == 0:
                        for pe_fn, ev_fn in mk_hproj(0):
                            ev_fn(pe_fn())
                    mk.op("act", lambda e: e.activation(Fh[:], Fh[:], AF.Sigmoid), [Fh], [Fh])
                    fl.step()
                    for i in range(4):
                        h = 4 * hh + i
                        mk.op("dve", lambda e, i=i, h=h: e.tensor_scalar(Fh[:, i, :], Fh[:, i, :], oml[:, h:h + 1], lb[:, h:h + 1], ALU.mult, ALU.add), [Fh, oml, lb], [Fh])
                        fl.step()
                    mk.op("act", lambda e: e.activation(Lh[:], Fh[:], AF.Ln), [Fh], [Lh])
                    for i in range(4):
                        mk.op("dve", lambda e, i=i: e.tensor_tensor_scan(Bh[:, i, :], cmask, Lh[:, i, :], 0.0, ALU.mult, ALU.add), [Lh, cst_f], [Bh])
                        fl.step()
                    mk.op("dve", lambda e: e.tensor_scalar(Fh[:], Fh[:], -1.0, 1.0, ALU.mult, ALU.add), [Fh], [Fh])
                    fl.step()
                    Bv = Bh[:].rearrange("p h (c t) -> p h c t", t=64)
                    mk.op("act", lambda e: e.activation(ebl[:], Bv[:, :, :, 63], AF.Exp), [Bh], [ebl])
                    mk.op("act", lambda e: e.activation(Lh[:], Bh[:], AF.Exp, scale=-1.0), [Bh], [Lh])
                    mk.op("dve", lambda e: e.tensor_tensor(Fh[:], Fh[:], Lh[:], ALU.mult), [Fh, Lh], [Fh])
                    fl.step()
                    mk.op("dve", lambda e: e.tensor_tensor(kdl[:].rearrange("p h (c t) -> p h c t", t=64), Fh[:].rearrange("p h (c t) -> p h c t", t=64),
                                                           ebl[:].unsqueeze(3).to_broadcast([128, 4, 8, 64]), ALU.mult), [Fh, ebl], [kdl])
                    fl.step()
                    mk.op("dve", lambda e: e.tensor_copy(kdo[:], Fh[:, :, 256:512]), [Fh], [kdo])
                    mk.op("act", lambda e: e.activation(Lh[:, :, 0:256], Bh[:, :, 256:512], AF.Exp), [Bh, kdl], [Lh])
                    mk.op("act", lambda e: e.activation(hq_f[:], hq_f[:], AF.Silu), [hq_f], [hq_f])
                    fl.step()
                    mk.op("dve", lambda e: e.tensor_tensor(qd[:], hq_f[:], Lh[:, :, 0:256], ALU.mult), [hq_f, Lh], [qd])
                    fl.flush()
                    if hh == 0:
                        mk.op("act", lambda e: e.activation(hg_f[:], hg_f[:], AF.Silu), [hg_f], [hg_f])
                        mk.op("act", lambda e: e.activation(cq_sq_v, cq_f[:], AF.Square), [cq_f], [sqtmp])
                        for c in range(4):
                            mk.op("pe", lambda e, c=c: e.matmul(ps_n[:, 0:HT], ones_b[:], cq_sq_v[:, c, :], start=(c == 0), stop=(c == 3)), [ones_b, sqtmp], [ps_n])
                        rstd_from_ps(ps_n, ps_n[:, 0:HT], rq, rq[:], rq, rq[:], 1.0 / 512)
                        for c in range(4):
                            mk.op("dve", lambda e, c=c: e.scalar_tensor_tensor(cqn[:, c, :], cq_f[:, c, :], gn[:, G_QN + c:G_QN + c + 1], rq[:], ALU.mult, ALU.mult), [cq_f, gn, rq], [cqn])
                    for src, dst in ((kdl, kdl_tok), (vT, v_tok)):
                        for tp in range(2):
                            for t2 in range(2):
                                tt = 2 * tp + t2
                                for i in range(4):
                                    mk.op("pe", lambda e, src=src, tt=tt, i=i, t2=t2: e.transpose(ps_tr[:, (t2 * 4 + i) * 128:(t2 * 4 + i + 1) * 128], src[:, i, tt * 128:(tt + 1) * 128], ident_b[:]),
                                          [src, ident_b], [ps_tr])
                            mk.op("act", lambda e, dst=dst, tp=tp: e.activation(dst[:, 2 * tp:2 * tp + 2, :, :], ps_tr[:, :].rearrange("p (t h d) -> p t h d", t=2, h=4), AF.Copy), [ps_tr], [dst])
                    fl2 = Filler(mk_hproj(1), 1) if hh == 0 else Filler([], 1)
                    for tt in range(4):
                        for c2 in range(2):
                            c = 2 * tt + c2
                            pr = slice(c2 * 64, c2 * 64 + 64)
                            bks = pst.next()
                            for i in range(4):
                                mk.op("pe", lambda e, i=i, tt=tt, pr=pr, bks=bks: e.matmul(bks[:, i * 128:(i + 1) * 128], kdl_tok[pr, tt, i, :], v_tok[pr, tt, i, :], start=True, stop=True), [kdl_tok, v_tok], [bks])
                            for i in range(4):
                                h = 4 * hh + i
                                if c >= 4:
                                    mk.op("act", lambda e, h=h, c=c: e.activation(st_snap[h][:, c - 4, :], st_f[h][:], AF.Copy), [st_f[h]], [st_snap[h]])
                                mk.op("dve", lambda e, h=h, i=i, c=c, bks=bks: e.scalar_tensor_tensor(st_f[h][:], st_f[h][:], ebl[:, i, c:c + 1], bks[:, i * 128:(i + 1) * 128], ALU.mult, ALU.add),
                                      [st_f[h], ebl, bks], [st_f[h]])
                            fl2.step()
                    for i in range(4):
                        h = 4 * hh + i
                        for tt in (2, 3):
                            oc0 = (tt - 2) * 128
                            bka = pst.next()
                            mk.op("pe", lambda e, i=i, oc0=oc0, bka=bka: e.matmul(bka[:, 0:128], kdo[:, i, oc0:oc0 + 128], qd[:, i, oc0:oc0 + 128], start=True, stop=True), [kdo, qd], [bka])
                            at = at_sb.next()
                            mk.op("dve", lambda e, at=at, bka=bka: e.tensor_tensor(at[:], bka[:, 0:128], tri_b[:], ALU.mult), [bka, tri_b], [at])
                            bko = pbo2.next()
                            mk.op("pe", lambda e, i=i, tt=tt, at=at, bko=bko: e.matmul(bko[:, 0:128], v_tok[:, tt, i, :], at[:], start=True, stop=False), [v_tok, at], [bko])
                            for c2 in range(2):
                                mk.op("pe", lambda e, i=i, h=h, oc0=oc0, c2=c2, tt=tt, bko=bko: e.matmul(bko[:, c2 * 64:c2 * 64 + 64], st_snap[h][:, 2 * tt + c2 - 4, :], qd[:, i, oc0 + c2 * 64:oc0 + c2 * 64 + 64],
                                                                                                       start=False, stop=(c2 == 1)), [st_snap[h], qd], [bko])
                            if tt == 2:
                                mk.op("act", lambda e, h=h, bko=bko: e.activation(o_f[:, h, 0:2], bko[:, 126:128], AF.Copy), [bko], [o_f])
                            else:
                                mk.op("act", lambda e, h=h, bko=bko: e.activation(o_f[:, h, 2:HT], bko[:, 0:128], AF.Copy), [bko], [o_f])
                            fl2.step()
                    fl2.flush()
                mk.op("act", lambda e: e.activation(o_sq_v, o_f[:], AF.Square), [o_f], [sqtmp])
                for hb in range(3):
                    hs = list(range(3 * hb, min(3 * hb + 3, 8)))
                    bk = pproj.next()
                    for k, h in enumerate(hs):
                        mk.op("pe", lambda e, k=k, h=h, bk=bk: e.matmul(bk[:, k * HT:(k + 1) * HT], ones_b[:], o_sq_v[:, h, :], start=True, stop=True), [ones_b, sqtmp], [bk])
                    n = len(hs)
                    rstd_from_ps(bk, bk[:, 0:n * HT].rearrange("p (h t) -> p h t", t=HT), ro, ro[:, hs[0]:hs[0] + n, :], ro, ro[:, hs[0]:hs[0] + n, :], 1.0 / 128)
                mk.op("dve", lambda e: e.scalar_tensor_tensor(o_f[:], o_f[:], gn[:, G_HGO:G_HGO + 1], ro[:], ALU.mult, ALU.mult), [o_f, gn, ro], [o_f])
                mk.op("dve", lambda e: e.tensor_tensor(r_b[:], o_f[:], hg_f[:], ALU.mult), [o_f, hg_f], [r_b])
                dma("pool", Rs[:, :, g * HT:(g + 1) * HT], r_b[:], [r_b], Rs)
                dma("pool", QNs[:, :, g * HT:(g + 1) * HT].rearrange("h p t -> p h t"), qn_sb[:], [qn_sb], QNs)
                dma("pool", QRs[:, :, g * HT:(g + 1) * HT].rearrange("h p t -> p h t"), qr_sb[:], [qr_sb], QRs)
                emit_conv(per_g)
            emit_conv(len(conv_jobs))
            mk.barrier()

        SC = 192.0 ** -0.5
        with ExitStack() as es:
            mk.stack = es
            pbs = Rot([mk.ps([128, 1024], F32, f"bs{i}") for i in range(2)])
            pbo2 = Rot([mk.ps([128, 512], F32, f"bo{i}") for i in range(2)])
            pbd2 = Rot([mk.ps([128, 512], F32, f"bd{i}") for i in range(2)])
            kt = mk.sb([128, S], BF16, "kt")
            vt = mk.sb([128, NT, 128], BF16, "vt")
            krs = mk.sb([128, S], BF16, "krs")
            qn = mk.sb([128, NO], BF16, "qn")
            qr = mk.sb([128, NO], BF16, "qr")
            a_h = mk.sb([128, NO], BF16, "a_h")
            W2 = 2 * HT
            pts = Rot([mk.sb([128, 2, W2], BF16, f"pt{i}") for i in range(4)])
            accs = Rot([mk.sb([128, 2, W2], F32, f"acc{i}") for i in range(2)])
            acc1s = Rot([mk.sb([128, 3, HT], F32, f"acd{i}") for i in range(2)])
            accb = Rot([mk.sb([128, 2, W2], BF16, f"accb{i}") for i in range(2)])
            acc1b = Rot([mk.sb([128, 3, HT], BF16, f"acdb{i}") for i in range(2)])
            rec = mk.sb([128, W2], F32, "rec")
            nseg = max(1, S // 4096)
            sw = S // nseg
            mk.op("dve", lambda e: e.memset(krs[64:128, :], 0.0), [], [krs])
            mk.op("dve", lambda e: e.memset(qr[64:128, :], 0.0), [], [qr])
            for sg in range(nseg):
                dma("sp", krs[0:64, sg * sw:(sg + 1) * sw], KRs[:, sg * sw:(sg + 1) * sw], [KRs], krs)
            for h in range(8):
                for sg in range(nseg):
                    dma("sp", kt[:, sg * sw:(sg + 1) * sw], KTs[h, :, sg * sw:(sg + 1) * sw], [KTs], kt, nowaw=(sg > 0))
                tw = NT // nseg
                for sg in range(nseg):
                    dma("sp", vt[:, sg * tw:(sg + 1) * tw, :], VTs[h, :, sg * tw:(sg + 1) * tw, :], [VTs], vt, nowaw=(sg > 0))
                dma("sp", qn[:], QNs[h], [QNs], qn, nowaw=False)
                dma("sp", qr[0:64, :], QRs[h], [QRs], qr, nowaw=False)
                items = []
                for u in range(NG // 2):
                    ncom = 8 * u + 4
                    for i in range(0, ncom, 2):
                        items.append((u, 0, [i, i + 1], i == 0, False))
                    items.append((u, 1, [ncom, ncom + 1, ncom + 2], False, False))
                    items.append((u, 1, [ncom + 3], False, True))

                def emit_qk(it):
                    u, kind, tiles, first, last = it
                    bs = pbs.next()
                    if kind == 0:
                        qs = slice(u * W2, (u + 1) * W2)
                        for i, ki in enumerate(tiles):
                            ks = slice(ki * 128, (ki + 1) * 128)
                            mk.op("pe", lambda e, bs=bs, i=i, ks=ks, qs=qs: e.matmul(bs[:, i * 512:i * 512 + W2], kt[:, ks], qn[:, qs], start=True, stop=False), [kt, qn], [bs])
                            mk.op("pe", lambda e, bs=bs, i=i, ks=ks, qs=qs: e.matmul(bs[:, i * 512:i * 512 + W2], krs[:, ks], qr[:, qs], start=False, stop=True), [krs, qr], [bs])
                    else:
                        qs = slice(u * W2 + HT, (u + 1) * W2)
                        for i, ki in enumerate(tiles):
                            ks = slice(ki * 128, (ki + 1) * 128)
                            mk.op("pe", lambda e, bs=bs, i=i, ks=ks, qs=qs: e.matmul(bs[:, i * HT:(i + 1) * HT], kt[:, ks], qn[:, qs], start=True, stop=False), [kt, qn], [bs])
                            mk.op("pe", lambda e, bs=bs, i=i, ks=ks, qs=qs: e.matmul(bs[:, i * HT:(i + 1) * HT], krs[:, ks], qr[:, qs], start=False, stop=True), [krs, qr], [bs])
                    return bs

                stE = {}
                stP = {}
                tailq = []

                def emit_exp(it, bs):
                    u, kind, tiles, first, last = it
                    ncom = 8 * u + 4
                    if first:
                        stE["acc"] = accs.next()
                        stE["acc1"] = acc1s.next()
                        acc0 = stE["acc"]
                        acc10 = stE["acc1"]
                        mk.op("dve", lambda e, acc0=acc0: e.memset(acc0[:], 0.0), [], [acc0])
                        mk.op("dve", lambda e, acc10=acc10: e.memset(acc10[:], 0.0), [], [acc10])
                    acc = stE["acc"]
                    acc1 = stE["acc1"]
                    p = pts.next()
                    if kind == 0:
                        bsv = bs[:, :].rearrange("p (b c) -> p b c", b=2)[:, :, 0:W2]
                        if tiles[0] < 3:
                            for i, ki in enumerate(tiles):
                                if ki < 3:
                                    kb = cst_f[:, K_KB + ki:K_KB + ki + 1]
                                    mk.op("act", lambda e, p=p, bs=bs, i=i, kb=kb: e.activation(p[:, i, :], bs[:, i * 512:i * 512 + W2], AF.Exp, bias=kb, scale=SC), [bs, cst_f], [p])
                                else:
                                    mk.op("act", lambda e, p=p, bs=bs, i=i: e.activation(p[:, i, :], bs[:, i * 512:i * 512 + W2], AF.Exp, scale=SC), [bs], [p])
                        else:
                            mk.op("act", lambda e, p=p, bsv=bsv: e.activation(p[:], bsv, AF.Exp, scale=SC), [bs], [p])
                        if tiles[1] == ncom - 1:
                            mk.op("dve", lambda e, p=p: e.tensor_tensor(p[:, 1, 0:HT], p[:, 1, 0:HT], dmask_b[:], ALU.mult), [p, dmask_b], [p])
                        mk.op("dve", lambda e, p=p, acc=acc: e.tensor_tensor(acc[:], acc[:], p[:], ALU.add), [acc, p], [acc])
                    else:
                        n = len(tiles)
                        pv = p[:].rearrange("p b c -> p (b c)")[:, 0:3 * HT].rearrange("p (t c) -> p t c", c=HT)
                        mk.op("act", lambda e, pv=pv, bs=bs, n=n: e.activation(pv[:, 0:n, :], bs[:, 0:n * HT].rearrange("p (t c) -> p t c", c=HT), AF.Exp, scale=SC), [bs], [p])
                        if last:
                            mk.op("dve", lambda e, pv=pv: e.tensor_tensor(pv[:, 0, :], pv[:, 0, :], dmask_b[:], ALU.mult), [p, dmask_b], [p])
                        mk.op("dve", lambda e, pv=pv, acc1=acc1, n=n: e.tensor_tensor(acc1[:, 0:n, :], acc1[:, 0:n, :], pv[:, 0:n, :], ALU.add), [acc1, p], [acc1])
                    if last:
                        ab = accb.next()
                        ab1 = acc1b.next()
                        mk.op("act", lambda e, ab=ab, acc=acc: e.activation(ab[:], acc[:], AF.Copy), [acc], [ab])
                        mk.op("act", lambda e, ab1=ab1, acc1=acc1: e.activation(ab1[:], acc1[:], AF.Copy), [acc1], [ab1])
                        tailq.append((ab, ab1))
                    return p

                def emit_pv(it, p):
                    u, kind, tiles, first, last = it
                    ncom = 8 * u + 4
                    nlast = ncom + 3
                    if first:
                        stP["bo"] = pbo2.next()
                    bo = stP["bo"]
                    if kind == 0:
                        for i, ki in enumerate(tiles):
                            mk.op("pe", lambda e, bo=bo, p=p, i=i, ki=ki: e.matmul(bo[:, 0:W2], vt[:, ki, :], p[:, i, :], start=(ki == 0), stop=False), [vt, p], [bo])
                    else:
                        pv = p[:].rearrange("p b c -> p (b c)")[:, 0:3 * HT].rearrange("p (t c) -> p t c", c=HT)
                        for i, ki in enumerate(tiles):
                            mk.op("pe", lambda e, bo=bo, pv=pv, i=i, ki=ki, nlast=nlast: e.matmul(bo[:, HT:W2], vt[:, ki, :], pv[:, i, :], start=False, stop=(ki == nlast)), [vt, p], [bo])
                    if last:
                        ab, ab1 = tailq.pop(0)
                        bd = pbd2.next()
                        for i in range(2):
                            mk.op("pe", lambda e, bd=bd, ab=ab, i=i: e.matmul(bd[:, 0:W2], ones_b[:], ab[:, i, :], start=(i == 0), stop=False), [ones_b, ab], [bd])
                        for i in range(3):
                            mk.op("pe", lambda e, bd=bd, ab1=ab1, i=i: e.matmul(bd[:, HT:W2], ones_b[:], ab1[:, i, :], start=False, stop=(i == 2)), [ones_b, ab1], [bd])
                        qs = slice(u * W2, (u + 1) * W2)
                        mk.op("dve", lambda e, bd=bd: e.tensor_scalar(rec[:], bd[:, 0:W2], 1e-30, None, ALU.max), [bd], [rec])
                        mk.op("dve", lambda e: e.reciprocal(rec[:], rec[:]), [rec], [rec])
                        mk.op("dve", lambda e, bo=bo, qs=qs: e.tensor_tensor(a_h[:, qs], bo[:, 0:W2], rec[:], ALU.mult), [bo, rec], [a_h])

                pend = []
                for it in items:
                    bs = emit_qk(it)
                    p = emit_exp(it, bs)
                    pend.append((it, p))
                    if len(pend) > 2:
                        emit_pv(*pend.pop(0))
                while pend:
                    emit_pv(*pend.pop(0))
                dma("pool", As[:, h, :], a_h[:], [a_h], As)
            mk.barrier()

        SCX = 512.0 ** -0.5
        G = 2 * HT
        NGC = NG // 2
        H2s = mk.dram("H2s", [128, KC, NO], F32)

        def make_helpers(pn, pj, wst, sqs):
            def ssq_rstd(src, src_ap, nchunk, width, out, inv_n, o0=0):
                for c in range(nchunk):
                    sq = sqs.next()
                    mk.op("act", lambda e, c=c, sq=sq: e.activation(sq[:, 0:width], src_ap(c), AF.Square), [src], [sq])
                    mk.op("pe", lambda e, c=c, sq=sq: e.matmul(pn[:, 0:width], ones_b[:], sq[:, 0:width], start=(c == 0), stop=(c == nchunk - 1)), [ones_b, sq], [pn])
                rstd_from_ps(pn, pn[:, 0:width], out, out[:, o0:o0 + width], out, out[:, o0:o0 + width], inv_n)

            def projw(Wb, oc, src, halves):
                wt = wst.next()
                dma("sp", wt[:], Wb[oc], [Wb], wt, nowaw=False)
                res = []
                for (c0, width) in halves:
                    bk = pj.next()
                    for kc in range(KC):
                        mk.op("pe", lambda e, kc=kc, bk=bk, c0=c0, width=width: e.matmul(bk[:, 0:width], wt[:, kc, :], src[:, kc, c0:c0 + width], start=(kc == 0), stop=(kc == KC - 1)), [wt, src], [bk])
                    res.append((bk, c0, width))
                return res
            return ssq_rstd, projw

        G1 = 4 * HT
        NG1 = NG // 4
        halves1 = [(0, 2 * HT), (2 * HT, 2 * HT)]
        with ExitStack() as es:
            mk.stack = es
            pj = Rot([mk.ps([128, 512], F32, f"pj{i}") for i in range(4)])
            pn = mk.ps([128, 512], F32, "pn")
            pa = Rot([mk.ps([128, 512], F32, f"pa{i}") for i in range(3)])
            hres = mk.sb([128, KC, G1], F32, "hres")
            y = mk.sb([128, KC, G1], F32, "y")
            xn = mk.sb([128, KC, G1], BF16, "xn")
            qx = mk.sb([128, KC, G1], BF16, "qx")
            sqs = Rot([mk.sb([128, G], BF16, f"sqs{i}") for i in range(3)])
            rs_a = mk.sb([128, G1], F32, "rs_a")
            rs_y = mk.sb([128, G1], F32, "rs_y")
            rs_h = mk.sb([128, G1], F32, "rs_h")
            recx = mk.sb([128, G], F32, "recx")
            pxs = Rot([mk.sb([128, G], BF16, f"px{i}") for i in range(4)])
            wst = Rot([mk.sb([128, KC, 128], BF16, f"wc{i}") for i in range(4)])
            kxT = mk.sb([128, KC, 256], BF16, "kxT")
            vx = mk.sb([128, 2, D], BF16, "vx")
            msq = mk.sb([128, KC, 256], BF16, "msq")
            wxv_t = mk.sb([128, KC, 512], BF16, "wxv_t")
            rcol = mk.sb([128, 2], F32, "rcol")
            ssq_rstd, projw = make_helpers(pn, pj, wst, sqs)

            def norm_residual(goff):
                for (h0, wd_) in halves1:
                    ssq_rstd(y, lambda c, h0=h0, wd_=wd_: y[:, c, h0:h0 + wd_], KC, wd_, rs_y, 1.0 / D, o0=h0)
                mk.op("dve", lambda e: e.tensor_tensor(y[:], y[:], gn[:, goff:goff + KC].unsqueeze(2).to_broadcast([128, KC, G1]), ALU.mult), [y, gn], [y])
                mk.op("dve", lambda e: e.tensor_tensor(y[:], y[:], rs_y[:].unsqueeze(1).to_broadcast([128, KC, G1]), ALU.mult), [y, rs_y], [y])
                mk.op("dve", lambda e: e.tensor_tensor(hres[:], hres[:], y[:], ALU.add), [y, hres], [hres])

            dma("sp", y[:, :, 0:256], memT[:].rearrange("(kc p) s -> p kc s", p=128), [memT], y, nowaw=False)
            mk.op("act", lambda e: e.activation(msq[:], y[:, :, 0:256], AF.Square), [y], [msq])
            for c in range(KC):
                mk.op("pe", lambda e, c=c: e.matmul(pn[:, 0:256], ones_b[:], msq[:, c, :], start=(c == 0), stop=(c == KC - 1)), [ones_b, msq], [pn])
            rstd_from_ps(pn, pn[:, 0:256], rs_y, rs_y[:, 0:256], rs_y, rs_y[:, 0:256], 1.0 / D)
            mk.op("dve", lambda e: e.tensor_tensor(xn[:, :, 0:256], y[:, :, 0:256], gn[:, G_MEM:G_MEM + KC].unsqueeze(2).to_broadcast([128, KC, 256]), ALU.mult), [y, gn], [xn])
            for oc in range(16):
                (bk, _, _), = projw(Wxk_b, oc, xn, [(0, 256)])
                mk.op("dve", lambda e, bk=bk, oc=oc: e.tensor_tensor(kxT[:, oc, :], bk[:, 0:256], rs_y[:, 0:256], ALU.mult), [bk, rs_y], [kxT])
            for ktm in range(2):
                for c in range(KC):
                    mk.op("pe", lambda e, c=c, ktm=ktm: e.matmul(pn[:, ktm:ktm + 1], msq[:, c, ktm * 128:(ktm + 1) * 128], ones_b[:, 0:1], start=(c == 0), stop=(c == KC - 1)), [msq, ones_b], [pn])
            rstd_from_ps(pn, pn[:, 0:2], rcol, rcol[:], rcol, rcol[:], 1.0 / D)
            for ob in range(4):
                dma("sp", wxv_t[:], Wxv_b[ob], [Wxv_b], wxv_t, nowaw=False)
                for ktm in range(2):
                    bk = pj.next()
                    for kc in range(KC):
                        mk.op("pe", lambda e, kc=kc, ktm=ktm, bk=bk: e.matmul(bk[:, :], xn[:, kc, ktm * 128:(ktm + 1) * 128], wxv_t[:, kc, :], start=(kc == 0), stop=(kc == KC - 1)), [xn, wxv_t], [bk])
                    mk.op("dve", lambda e, bk=bk, ktm=ktm, ob=ob: e.tensor_scalar(vx[:, ktm, ob * 512:(ob + 1) * 512], bk[:, :], rcol[:, ktm:ktm + 1], None, ALU.mult), [bk, rcol], [vx])

            xTv = xT[:].rearrange("(kc p) s -> p kc s", p=128)
            for gc in range(NG1):
                c0 = gc * G1
                for t2 in range(4):
                    m = 4 * gc + t2
                    for q in range(2):
                        dma("sp", hres[:, 8 * q:8 * q + 8, t2 * HT:(t2 + 1) * HT], xTv[:, 8 * q:8 * q + 8, 512 * m + 382:512 * m + 512], [xT], hres, nowaw=(t2 + q > 0))
                dma("sp", xn[:, 0:8, :], As[:, :, c0:c0 + G1], [As], xn, nowaw=False)
                dma("sp", xn[:, 8:16, :], Rs[:, :, c0:c0 + G1], [Rs], xn, nowaw=True)
                for (h0, wd_) in halves1:
                    ssq_rstd(xn, lambda c, h0=h0, wd_=wd_: xn[:, c, h0:h0 + wd_], 8, wd_, rs_a, 1.0 / 1024, o0=h0)
                mk.op("dve", lambda e: e.tensor_tensor(xn[:, 0:8, :], xn[:, 0:8, :], gn[:, G_MLAO:G_MLAO + 8].unsqueeze(2).to_broadcast([128, 8, G1]), ALU.mult), [xn, gn], [xn])
                mk.op("dve", lambda e: e.tensor_tensor(xn[:, 0:8, :], xn[:, 0:8, :], rs_a[:].unsqueeze(1).to_broadcast([128, 8, G1]), ALU.mult), [xn, rs_a], [xn])
                for oc in range(16):
                    for (bk, h0, wd_) in projw(Wout_b, oc, xn, halves1):
                        mk.op("act", lambda e, bk=bk, oc=oc, h0=h0, wd_=wd_: e.activation(y[:, oc, h0:h0 + wd_], bk[:, 0:wd_], AF.Copy), [bk], [y])
                norm_residual(G_MIXPOST)
                for (h0, wd_) in halves1:
                    ssq_rstd(hres, lambda c, h0=h0, wd_=wd_: hres[:, c, h0:h0 + wd_], KC, wd_, rs_h, 1.0 / D, o0=h0)
                mk.op("dve", lambda e: e.tensor_tensor(xn[:], hres[:], gn[:, G_XPRE:G_XPRE + KC].unsqueeze(2).to_broadcast([128, KC, G1]), ALU.mult), [hres, gn], [xn])
                for oc in range(16):
                    for (bk, h0, wd_) in projw(Wxq_b, oc, xn, halves1):
                        mk.op("dve", lambda e, bk=bk, oc=oc, h0=h0, wd_=wd_: e.tensor_tensor(qx[:, oc, h0:h0 + wd_], bk[:, 0:wd_], rs_h[:, h0:h0 + wd_], ALU.mult), [bk, rs_h], [qx])
                for hx in range(4):
                    for (h0, wd_) in halves1:
                        ps_ = []
                        for ktm in range(2):
                            bs = pa.next()
                            for dc in range(4):
                                mk.op("pe", lambda e, bs=bs, dc=dc, ktm=ktm, hx=hx, h0=h0, wd_=wd_: e.matmul(bs[:, 0:wd_], kxT[:, 4 * hx + dc, ktm * 128:(ktm + 1) * 128], qx[:, 4 * hx + dc, h0:h0 + wd_], start=(dc == 0), stop=(dc == 3)), [kxT, qx], [bs])
                            p = pxs.next()
                            mk.op("act", lambda e, p=p, bs=bs, wd_=wd_: e.activation(p[:, 0:wd_], bs[:, 0:wd_], AF.Exp, scale=SCX), [bs], [p])
                            ps_.append(p)
                        for ktm in range(2):
                            mk.op("pe", lambda e, ktm=ktm, p=ps_[ktm], wd_=wd_: e.matmul(pn[:, 0:wd_], ones_b[:], p[:, 0:wd_], start=(ktm == 0), stop=(ktm == 1)), [ones_b, ps_[ktm]], [pn])
                        mk.op("dve", lambda e, wd_=wd_: e.reciprocal(recx[:, 0:wd_], pn[:, 0:wd_]), [pn], [recx])
                        for dc in range(4):
                            bo = pa.next()
                            for ktm in range(2):
                                mk.op("pe", lambda e, bo=bo, ktm=ktm, dc=dc, hx=hx, p=ps_[ktm], wd_=wd_: e.matmul(bo[:, 0:wd_], vx[:, ktm, (4 * hx + dc) * 128:(4 * hx + dc + 1) * 128], p[:, 0:wd_], start=(ktm == 0), stop=(ktm == 1)), [vx, ps_[ktm]], [bo])
                            mk.op("dve", lambda e, bo=bo, dc=dc, hx=hx, h0=h0, wd_=wd_: e.tensor_tensor(xn[:, 4 * hx + dc, h0:h0 + wd_], bo[:, 0:wd_], recx[:, 0:wd_], ALU.mult), [bo, recx], [xn])
                for oc in range(16):
                    for (bk, h0, wd_) in projw(Wxo_b, oc, xn, halves1):
                        mk.op("act", lambda e, bk=bk, oc=oc, h0=h0, wd_=wd_: e.activation(y[:, oc, h0:h0 + wd_], bk[:, 0:wd_], AF.Copy), [bk], [y])
                norm_residual(G_XPOST)
                for q in range(2):
                    dma("pool", H2s[:, 8 * q:8 * q + 8, c0:c0 + G1], hres[:, 8 * q:8 * q + 8, :], [hres], H2s)
            mk.barrier()

        G2 = 4 * HT
        NG2 = NG // 4
        with ExitStack() as es:
            mk.stack = es
            pj = Rot([mk.ps([128, 512], F32, f"qj{i}") for i in range(4)])
            pn = mk.ps([128, 512], F32, "qn_")
            pd = Rot([mk.ps([128, 512], F32, f"qd{i}") for i in range(2)])
            hres = mk.sb([128, KC, G2], F32, "hres2")
            y = mk.sb([128, KC, 512], F32, "y2")
            xn = mk.sb([128, KC, G2], BF16, "xn2")
            sqs = Rot([mk.sb([128, G2], BF16, f"sqq{i}") for i in range(2)])
            rs_y = mk.sb([128, 512], F32, "rs_y2")
            rs_h = mk.sb([128, G2], F32, "rs_h2")
            actb = mk.sb([128, NFC, 512], BF16, "actb")
            ugs = Rot([mk.sb([128, G2], F32, f"ug{i}") for i in range(2)])
            uvs = Rot([mk.sb([128, G2], F32, f"uv{i}") for i in range(2)])
            cgs = Rot([mk.sb([128, 4, 128], F32, f"cg{i}") for i in range(2)])
            cvs = Rot([mk.sb([128, 4, 128], F32, f"cv{i}") for i in range(2)])
            wst = Rot([mk.sb([128, KC, 128], BF16, f"wu{i}") for i in range(3)])
            wdn = Rot([mk.sb([128, NFC, 128], BF16, f"wd{i}") for i in range(2)])
            ssq_rstd, projw = make_helpers(pn, pj, wst, sqs)
            outTv = outT[:].rearrange("(kc p) s -> p kc s", p=128)
            halves = [(0, 2 * HT), (2 * HT, 2 * HT)]
            for g2 in range(NG2):
                c0 = g2 * G2
                for q in range(4):
                    dma("sp", hres[:, 4 * q:4 * q + 4, :], H2s[:, 4 * q:4 * q + 4, c0:c0 + G2], [H2s], hres, nowaw=(q > 0))
                for (h0, wd_) in halves:
                    ssq_rstd(hres, lambda c, h0=h0, wd_=wd_: hres[:, c, h0:h0 + wd_], KC, wd_, rs_h, 1.0 / D, o0=h0)
                mk.op("dve", lambda e: e.tensor_tensor(xn[:], hres[:], gn[:, G_FFNPRE:G_FFNPRE + KC].unsqueeze(2).to_broadcast([128, KC, G2]), ALU.mult), [hres, gn], [xn])
                for c in range(NFC):
                    outs = []
                    for (ci, ubuf, cbuf) in ((c, ugs, cgs), (NFC + c, uvs, cvs)):
                        u = ubuf.next()
                        for (bk, h0, wd_) in projw(Wup_b, ci, xn, halves):
                            mk.op("dve", lambda e, bk=bk, u=u, h0=h0, wd_=wd_: e.tensor_tensor(u[:, h0:h0 + wd_], bk[:, 0:wd_], rs_h[:, h0:h0 + wd_], ALU.mult), [bk, rs_h], [u])
                        if g2 == 0:
                            mk.op("dve", lambda e, u=u: e.tensor_scalar(u[:, 0:2], u[:, 0:2], cst_f[:, K_H0:K_H0 + 1], None, ALU.mult), [u, cst_f], [u])
                        cv = cbuf.next()
                        uv3 = u[:].rearrange("p (t c) -> p t c", c=HT)
                        w0 = gn[:, G_CW + ci:G_CW + ci + 1]
                        w1 = gn[:, G_CW + 88 + ci:G_CW + 88 + ci + 1]
                        w2 = gn[:, G_CW + 176 + ci:G_CW + 176 + ci + 1]
                        bb = gn[:, G_CB + ci:G_CB + ci + 1]
                        mk.op("dve", lambda e, cv=cv, uv3=uv3, w2=w2, bb=bb: e.tensor_scalar(cv[:], uv3[:, :, 2:HT], w2, bb, ALU.mult, ALU.add), [u, gn], [cv])
                        mk.op("dve", lambda e, cv=cv, uv3=uv3, w1=w1: e.scalar_tensor_tensor(cv[:], uv3[:, :, 1:HT - 1], w1, cv[:], ALU.mult, ALU.add), [u, gn, cv], [cv])
                        mk.op("dve", lambda e, cv=cv, uv3=uv3, w0=w0: e.scalar_tensor_tensor(cv[:], uv3[:, :, 0:HT - 2], w0, cv[:], ALU.mult, ALU.add), [u, gn, cv], [cv])
                        outs.append(cv)
                    cg, cvv = outs
                    mk.op("act", lambda e, cg=cg: e.activation(cg[:], cg[:], AF.Gelu_apprx_tanh), [cg], [cg])
                    mk.op("dve", lambda e, cg=cg, cvv=cvv, c=c: e.tensor_tensor(actb[:, c, :].rearrange("p (t c) -> p t c", c=128), cg[:], cvv[:], ALU.mult), [cg, cvv], [actb])
                for oc in range(16):
                    wt = wdn.next()
                    for ch in range(2):
                        dma("sp", wt[:, 22 * ch:22 * ch + 22, :], Wdn_b[oc, :, 22 * ch:22 * ch + 22, :], [Wdn_b], wt, nowaw=(ch > 0))
                    bk = pd.next()
                    for c in range(NFC):
                        mk.op("pe", lambda e, bk=bk, wt=wt, c=c: e.matmul(bk[:, :], wt[:, c, :], actb[:, c, :], start=(c == 0), stop=(c == NFC - 1)), [wt, actb], [bk])
                    mk.op("act", lambda e, bk=bk, oc=oc: e.activation(y[:, oc, :], bk[:, :], AF.Copy), [bk], [y])
                hown = hres[:].rearrange("p k (t c) -> p k t c", c=HT)[:, :, :, 2:HT]
                ssq_rstd(y, lambda c: y[:, c, :], KC, 512, rs_y, 1.0 / D)
                yv = y[:, :, :]
                mk.op("dve", lambda e: e.tensor_tensor(yv, yv, gn[:, G_FFNPOST:G_FFNPOST + KC].unsqueeze(2).to_broadcast([128, KC, 512]), ALU.mult), [y, gn], [y])
                mk.op("dve", lambda e: e.tensor_tensor(yv, yv, rs_y[:].unsqueeze(1).to_broadcast([128, KC, 512]), ALU.mult), [y, rs_y], [y])
                mk.op("dve", lambda e: e.tensor_tensor(yv.rearrange("p k (t c) -> p k t c", c=128), hown, yv.rearrange("p k (t c) -> p k t c", c=128), ALU.add), [y, hres], [y])
                for q in range(4):
                    dma("pool", outTv[:, 4 * q:4 * q + 4, g2 * 512:(g2 + 1) * 512], y[:, 4 * q:4 * q + 4, :], [y], outT)
            mk._wait("sp", *outT.lw)
            mk._wait("pool", *outT.lw)
    return nc, mk


def host_inputs(inp, NG, core):
    S = NG * 512
    b, j = core // 4, core % 4
    pad = (3 - j) * 128
    x = np.asarray(inp["x"], dtype=np.float32)
    xT = np.zeros((D, S), np.float32)
    nreal = S - pad
    xT[:, pad:] = x[b, :nreal, :].T
    d = {"xT": xT, "memT": np.ascontiguousarray(np.asarray(inp["mem"], np.float32)[b].T)}
    pos = (np.arange(S) - pad).astype(np.float32)
    inv = (1.0 / (10000.0 ** (np.arange(0, 64, 2, dtype=np.float32) / np.float32(64)))).astype(np.float32)
    ang = pos[None, :] * inv[:, None]
    cs = np.zeros((2, 64, S), np.float32)
    cs[0, :32] = np.cos(ang); cs[0, 32:] = np.cos(ang)
    cs[1, :32] = np.sin(ang); cs[1, 32:] = np.sin(ang)
    d["cs"] = cs
    cst = np.zeros((128, NCST), np.float32)
    cst[:, K_ID:K_ID + 128] = np.eye(128)
    cm = np.ones(512, np.float32); cm[::64] = 0
    cst[:, K_CM:K_CM + 512] = cm[None, :]
    s_i = np.arange(128)[:, None]; t_i = np.arange(128)[None, :]
    cst[:, K_TRI:K_TRI + 128] = ((s_i // 64 == t_i // 64) & (s_i <= t_i)).astype(np.float32)
    qc = np.concatenate([[-1, -1], np.arange(128) // 64])[None, :]
    cst[:, K_DM:K_DM + HT] = ((s_i // 64) <= qc).astype(np.float32)
    for t in range(3):
        cst[:, K_KB + t] = -30000.0 if t < 3 - j else 0.0
    cst[:, K_H0] = 0.0 if j == 0 else 1.0
    d["cst"] = cst
    gn = np.zeros((128, NGN), np.float32)

    def pk(v):
        v = np.asarray(v, np.float32).reshape(-1)
        return v.reshape(-1, 128).T

    for name, off in (("ln_mix_pre", G_MIXPRE), ("ln_mix_post", G_MIXPOST), ("ln_x_pre", G_XPRE), ("ln_x_post", G_XPOST),
                      ("mem_norm", G_MEM), ("ln_ffn_pre", G_FFNPRE), ("ln_ffn_post", G_FFNPOST), ("q_norm", G_QN),
                      ("kv_norm", G_KVN), ("mla_out_norm", G_MLAO), ("hgrn_out_norm", G_HGO)):
        a = pk(inp[name][0])
        gn[:, off:off + a.shape[1]] = a
    lbv = np.asarray(inp["hgrn_lb"], np.float32)
    gn[:, G_LB0:G_LB0 + 8] = pk(lbv[0]); gn[:, G_LB1:G_LB1 + 8] = pk(lbv[1])
    cw = np.asarray(inp["conv_w"], np.float32)[0]
    for k in range(3):
        gn[:, G_CW + 88 * k:G_CW + 88 * (k + 1)] = pk(cw[k])
    gn[:, G_CB:G_CB + 88] = pk(inp["conv_b"][0])
    d["gains"] = gn
    for name in ("w_in", "w_uq", "w_ukv", "w_out", "w_xq", "w_xk", "w_xv", "w_xo", "w_up", "w_down"):
        d[name] = np.ascontiguousarray(np.asarray(inp[name], np.float32)[0])
    return d


_CACHE = {}


def kernel(**inputs):
    NG = 32
    if NG not in _CACHE:
        _CACHE[NG] = build(NG)[0]
    nc = _CACHE[NG]
    in_maps = [host_inputs(inputs, NG, c) for c in range(8)]
    res = run_bass_kernel_spmd(nc, in_maps, core_ids=list(range(8)))
    out = np.zeros((2, 16384, D), np.float32)
    for c in range(8):
        b, j = c // 4, c % 4
        o = res.results[c]["outT"]
        o = o.reshape(D, NG, 128)
        for m in range(NG):
            i = 4 * m + j
            out[b, i * 128:(i + 1) * 128, :] = o[:, m, :].T
    return out
```
